# Optimizing a Trainium2 kernel written in Bass

```python
import math
import jax, jax.numpy as jnp
from jax import lax
import numpy as np

D_MODEL = 1024
BATCH = 8
SEQ = 2048
DEPTH = 4
DEC_BATCH = 32
DEC_SEQ = 1
PAST_LEN = 8192
PAGE_SIZE = 128

N_A_LAYERS = DEPTH // 2
N_B_LAYERS = DEPTH - N_A_LAYERS
SSM_GROUP = 16
N_SSM_GROUPS = D_MODEL // SSM_GROUP
SSM_STATE = 64
DT_MIN = 1e-3
DT_MAX = 1e-1
DIL_WINDOWS = (128, 512, 2048)
DIL_RATES = (1, 4, 16)
N_DIL = 3
HEADS_PER_GROUP = 8
HEAD_DIM = 64
ATTN_WIDTH = HEADS_PER_GROUP * HEAD_DIM
D_FF = 2816
EPS = 1e-6
NEG = -1e30

kernel_name = 'yoco_s5_dilated_alibi_macaron'


def _rmsnorm(x, g):
    xf = x.astype(jnp.float32)
    y = xf * lax.rsqrt(jnp.mean(xf * xf, axis=-1, keepdims=True) + EPS)
    return (y * g.astype(jnp.float32)).astype(x.dtype)


def _swiglu(x, w_in, w_out):
    hcat = x @ w_in
    return (jax.nn.silu(hcat[..., :D_FF]) * hcat[..., D_FF:]) @ w_out


def _alibi_slopes():
    n = N_DIL * HEADS_PER_GROUP
    s = jnp.exp2(-8.0 * jnp.arange(1, n + 1, dtype=jnp.float32) / n)
    return s.reshape(N_DIL, HEADS_PER_GROUP)


def _cmul_combine(e1, e2):
    a1r, a1i, b1r, b1i = e1
    a2r, a2i, b2r, b2i = e2
    return (a2r * a1r - a2i * a1i,
            a2r * a1i + a2i * a1r,
            a2r * b1r - a2i * b1i + b2r,
            a2r * b1i + a2i * b1r + b2i)


def _s5_mixer(u, s0_re, s0_im, log_dt, lam_re, lam_im, b_re, b_im, c_re, c_im, d_skip, glu_w, glu_b):
    f32 = jnp.float32
    bn, t_len, _ = u.shape
    ug = u.astype(f32).reshape(bn, t_len, N_SSM_GROUPS, SSM_GROUP)
    dt = jnp.exp(log_dt.astype(f32))[:, None]
    lr = lam_re.astype(f32)
    li = lam_im.astype(f32)
    mag = jnp.exp(lr * dt)
    ar = mag * jnp.cos(li * dt)
    ai = mag * jnp.sin(li * dt)
    den = lr * lr + li * li
    cr = ((ar - 1.0) * lr + ai * li) / den
    ci = (ai * lr - (ar - 1.0) * li) / den
    br = b_re.astype(f32)
    bi = b_im.astype(f32)
    bbr = cr[..., None] * br - ci[..., None] * bi
    bbi = cr[..., None] * bi + ci[..., None] * br
    bu_r = jnp.einsum('btgh,gph->tbgp', ug, bbr)
    bu_i = jnp.einsum('btgh,gph->tbgp', ug, bbi)
    if s0_re is not None:
        s0r = s0_re.astype(f32)
        s0i = s0_im.astype(f32)
        bu_r = bu_r.at[0].add(ar * s0r - ai * s0i)
        bu_i = bu_i.at[0].add(ar * s0i + ai * s0r)
    a_r = jnp.broadcast_to(ar, (t_len, 1) + ar.shape)
    a_i = jnp.broadcast_to(ai, (t_len, 1) + ai.shape)
    _, _, xr, xi = lax.associative_scan(_cmul_combine, (a_r, a_i, bu_r, bu_i), axis=0)
    y = (jnp.einsum('tbgp,ghp->btgh', xr, c_re.astype(f32))
         - jnp.einsum('tbgp,ghp->btgh', xi, c_im.astype(f32))
         + d_skip.astype(f32) * ug)
    act = jax.nn.gelu(y.reshape(bn, t_len, D_MODEL))
    z = act @ glu_w + glu_b
    out = z[..., :D_MODEL] * jax.nn.sigmoid(z[..., D_MODEL:])
    return out, xr[-1], xi[-1]


def _to_strided(t, n_res, nb, r_len, dil):
    bn, s_len = t.shape[:2]
    rest = t.shape[2:]
    pad_rest = [(0, 0)] * len(rest)
    t = jnp.pad(t, [(0, 0), (0, n_res * dil - s_len)] + pad_rest)
    t = jnp.moveaxis(t.reshape((bn, n_res, dil) + rest), 2, 1)
    t = jnp.pad(t, [(0, 0), (0, 0), (0, nb * r_len - n_res)] + pad_rest)
    return t.reshape((bn, dil, nb, r_len) + rest)


def _from_strided(t, s_len, n_res):
    bn, dil, nb, r_len = t.shape[:4]
    rest = t.shape[4:]
    t = t.reshape((bn, dil, nb * r_len) + rest)[:, :, :n_res]
    t = jnp.moveaxis(t, 1, 2)
    return t.reshape((bn, n_res * dil) + rest)[:, :s_len]


def _dilated_attn_prompt(q, k, v, dil, window, slopes):
    f32 = jnp.float32
    bn, s_len, n_h, e = q.shape
    r_len = window // dil
    n_res = -(-s_len // dil)
    nb = -(-n_res // r_len)
    qs = _to_strided(q.astype(f32), n_res, nb, r_len, dil)
    ks = _to_strided(k.astype(f32), n_res, nb, r_len, dil)
    vs = _to_strided(v.astype(f32), n_res, nb, r_len, dil)

    def with_prev(t):
        prev = jnp.pad(t, [(0, 0), (0, 0), (1, 0)] + [(0, 0)] * 3)[:, :, :-1]
        return jnp.concatenate([prev, t], axis=3)

    kb = with_prev(ks)
    vb = with_prev(vs)
    s = jnp.einsum('bdnqhe,bdnkhe->bdnqhk', qs, kb) * (e ** -0.5)
    qi = jnp.arange(r_len)[:, None]
    kj = jnp.arange(2 * r_len)[None, :]
    diff = r_len + qi - kj
    kidx = (jnp.arange(nb)[:, None, None] - 1) * r_len + kj[None]
    valid = (diff >= 0) & (diff <= r_len) & (kidx >= 0)
    bias = -slopes[None, :, None] * (dil * diff).astype(f32)[:, None, :]
    s = jnp.where(valid[:, :, None, :], s + bias, NEG)
    m = jnp.max(s, axis=-1)
    p = jnp.exp(s - m[..., None])
    l = jnp.sum(p, axis=-1)
    acc = jnp.einsum('bdnqhk,bdnkhe->bdnqhe', p, vb)
    return (_from_strided(acc, s_len, n_res), _from_strided(m, s_len, n_res),
            _from_strided(l, s_len, n_res))


def _dilated_attn_sample(q, k_new, v_new, k_buf, v_buf, dil, window, slopes):
    f32 = jnp.float32
    e = q.shape[-1]
    t_len = q.shape[1]
    lb = k_buf.shape[1]
    r_len = window // dil
    kc = jnp.concatenate([k_buf.astype(f32), k_new.astype(f32)], axis=1)
    vc = jnp.concatenate([v_buf.astype(f32), v_new.astype(f32)], axis=1)
    ti = jnp.arange(t_len)[:, None]
    kk = jnp.arange(r_len + 1)[None, :]
    idx = lb + ti - kk * dil
    valid = idx >= 0
    idxc = jnp.maximum(idx, 0)
    kg = kc[:, idxc]
    vg = vc[:, idxc]
    s = jnp.einsum('bthe,btkhe->bthk', q.astype(f32), kg) * (e ** -0.5)
    bias = -slopes[:, None] * (dil * kk).astype(f32)
    s = jnp.where(valid[:, None, :], s + bias, NEG)
    m = jnp.max(s, axis=-1)
    p = jnp.exp(s - m[..., None])
    l = jnp.sum(p, axis=-1)
    acc = jnp.einsum('bthk,btkhe->bthe', p, vg)
    return acc, m, l


def _dilated_mixer(u, kv_groups, kv_bufs, w_q, w_o):
    bn, t_len, _ = u.shape
    q = (u @ w_q).reshape(bn, t_len, N_DIL, HEADS_PER_GROUP, HEAD_DIM)
    slopes = _alibi_slopes()
    accs, ms, ls = [], [], []
    for g in range(N_DIL):
        k = kv_groups[g][:, :, 0]
        v = kv_groups[g][:, :, 1]
        if kv_bufs is None:
            acc, m, l = _dilated_attn_prompt(q[:, :, g], k, v, DIL_RATES[g], DIL_WINDOWS[g], slopes[g])
        else:
            acc, m, l = _dilated_attn_sample(q[:, :, g], k, v, kv_bufs[g][:, :, 0], kv_bufs[g][:, :, 1],
                                             DIL_RATES[g], DIL_WINDOWS[g], slopes[g])
        accs.append(acc)
        ms.append(m)
        ls.append(l)
    m_all = jnp.stack(ms)
    l_all = jnp.stack(ls)
    acc_all = jnp.stack(accs)
    w = jnp.exp(m_all - jnp.max(m_all, axis=0, keepdims=True))
    den = jnp.sum(w * l_all, axis=0)
    merged = jnp.sum(w[..., None] * acc_all, axis=0) / den[..., None]
    return merged.reshape(bn, t_len, ATTN_WIDTH) @ w_o


def _trunk(x, ssm_re0, ssm_im0, kv_bufs, weights):
    (norm_ffn1, ffn1_w_in, ffn1_w_out, norm_mix, norm_ffn2, ffn2_w_in, ffn2_w_out, norm_kv, norm_final,
     ssm_log_dt, ssm_lambda_re, ssm_lambda_im, ssm_b_re, ssm_b_im, ssm_c_re, ssm_c_im, ssm_d, glu_w, glu_b,
     attn_w_q, attn_w_kv, attn_w_o) = weights
    bn, t_len, _ = x.shape
    h = x
    ssm_re, ssm_im = [], []
    kv_groups = None
    for layer in range(DEPTH):
        if layer == N_A_LAYERS:
            kv = (_rmsnorm(h, norm_kv) @ attn_w_kv).reshape(bn, t_len, N_DIL, 2, HEADS_PER_GROUP, HEAD_DIM)
            kv_groups = [kv[:, :, g] for g in range(N_DIL)]
        h = h + (0.5 * _swiglu(_rmsnorm(h, norm_ffn1[layer]), ffn1_w_in[layer], ffn1_w_out[layer])).astype(h.dtype)
        u = _rmsnorm(h, norm_mix[layer])
        if layer < N_A_LAYERS:
            s0r = None if ssm_re0 is None else ssm_re0[layer]
            s0i = None if ssm_im0 is None else ssm_im0[layer]
            mix, sr, si = _s5_mixer(u, s0r, s0i, ssm_log_dt[layer], ssm_lambda_re[layer], ssm_lambda_im[layer],
                                    ssm_b_re[layer], ssm_b_im[layer], ssm_c_re[layer], ssm_c_im[layer],
                                    ssm_d[layer], glu_w[layer], glu_b[layer])
            ssm_re.append(sr)
            ssm_im.append(si)
        else:
            b = layer - N_A_LAYERS
            mix = _dilated_mixer(u, kv_groups, kv_bufs, attn_w_q[b], attn_w_o[b])
        h = h + mix.astype(h.dtype)
        h = h + (0.5 * _swiglu(_rmsnorm(h, norm_ffn2[layer]), ffn2_w_in[layer], ffn2_w_out[layer])).astype(h.dtype)
    return _rmsnorm(h, norm_final), jnp.stack(ssm_re), jnp.stack(ssm_im), kv_groups


def setup_inputs(seed: int = 0) -> dict:
    key = jax.random.key(seed)
    ks = jax.random.split(key, 32)
    f32 = jnp.float32
    nrm = lambda k, shape, s: jax.random.normal(k, shape, f32) * s
    out = {}
    out['x_prompt'] = nrm(ks[0], (BATCH, SEQ, D_MODEL), 1.0)
    out['x_sample'] = nrm(ks[1], (DEC_BATCH, DEC_SEQ, D_MODEL), 1.0)
    out['state_ssm_re'] = nrm(ks[2], (N_A_LAYERS, DEC_BATCH, N_SSM_GROUPS, SSM_STATE), 0.3)
    out['state_ssm_im'] = nrm(ks[3], (N_A_LAYERS, DEC_BATCH, N_SSM_GROUPS, SSM_STATE), 0.3)
    out['cache_kv_w128'] = nrm(ks[4], (DEC_BATCH, min(DIL_WINDOWS[0], PAST_LEN), 2, HEADS_PER_GROUP, HEAD_DIM), 1.0)
    out['cache_kv_w512'] = nrm(ks[5], (DEC_BATCH, min(DIL_WINDOWS[1], PAST_LEN), 2, HEADS_PER_GROUP, HEAD_DIM), 1.0)
    out['cache_kv_w2048'] = nrm(ks[6], (DEC_BATCH, min(DIL_WINDOWS[2], PAST_LEN), 2, HEADS_PER_GROUP, HEAD_DIM), 1.0)
    out['norm_ffn1'] = 1.0 + nrm(ks[7], (DEPTH, D_MODEL), 0.02)
    out['ffn1_w_in'] = nrm(ks[8], (DEPTH, D_MODEL, 2 * D_FF), D_MODEL ** -0.5)
    out['ffn1_w_out'] = nrm(ks[9], (DEPTH, D_FF, D_MODEL), D_FF ** -0.5)
    out['norm_mix'] = 1.0 + nrm(ks[10], (DEPTH, D_MODEL), 0.02)
    out['norm_ffn2'] = 1.0 + nrm(ks[11], (DEPTH, D_MODEL), 0.02)
    out['ffn2_w_in'] = nrm(ks[12], (DEPTH, D_MODEL, 2 * D_FF), D_MODEL ** -0.5)
    out['ffn2_w_out'] = nrm(ks[13], (DEPTH, D_FF, D_MODEL), D_FF ** -0.5)
    out['norm_kv'] = 1.0 + nrm(ks[14], (D_MODEL,), 0.02)
    out['norm_final'] = 1.0 + nrm(ks[15], (D_MODEL,), 0.02)
    out['ssm_log_dt'] = jax.random.uniform(ks[16], (N_A_LAYERS, N_SSM_GROUPS), f32,
                                           math.log(DT_MIN), math.log(DT_MAX))
    out['ssm_lambda_re'] = -0.5 + nrm(ks[17], (N_A_LAYERS, N_SSM_GROUPS, SSM_STATE), 0.01)
    out['ssm_lambda_im'] = (math.pi * jnp.arange(SSM_STATE, dtype=f32))[None, None, :] + nrm(
        ks[18], (N_A_LAYERS, N_SSM_GROUPS, SSM_STATE), 0.01)
    b_scale = (2.0 * SSM_GROUP) ** -0.5
    out['ssm_b_re'] = nrm(ks[19], (N_A_LAYERS, N_SSM_GROUPS, SSM_STATE, SSM_GROUP), b_scale)
    out['ssm_b_im'] = nrm(ks[20], (N_A_LAYERS, N_SSM_GROUPS, SSM_STATE, SSM_GROUP), b_scale)
    out['ssm_c_re'] = nrm(ks[21], (N_A_LAYERS, N_SSM_GROUPS, SSM_GROUP, SSM_STATE), 0.5)
    out['ssm_c_im'] = nrm(ks[22], (N_A_LAYERS, N_SSM_GROUPS, SSM_GROUP, SSM_STATE), 0.5)
    out['ssm_d'] = nrm(ks[23], (N_A_LAYERS, N_SSM_GROUPS, SSM_GROUP), 1.0)
    out['glu_w'] = nrm(ks[24], (N_A_LAYERS, D_MODEL, 2 * D_MODEL), D_MODEL ** -0.5)
    out['glu_b'] = nrm(ks[25], (N_A_LAYERS, 2 * D_MODEL), 0.01)
    out['attn_w_q'] = nrm(ks[26], (N_B_LAYERS, D_MODEL, N_DIL * ATTN_WIDTH), D_MODEL ** -0.5)
    out['attn_w_kv'] = nrm(ks[27], (D_MODEL, N_DIL * 2 * ATTN_WIDTH), D_MODEL ** -0.5)
    out['attn_w_o'] = nrm(ks[28], (N_B_LAYERS, ATTN_WIDTH, D_MODEL), ATTN_WIDTH ** -0.5)
    return out


def reference(x_prompt, x_sample, state_ssm_re, state_ssm_im, cache_kv_w128, cache_kv_w512, cache_kv_w2048,
              norm_ffn1, ffn1_w_in, ffn1_w_out, norm_mix, norm_ffn2, ffn2_w_in, ffn2_w_out, norm_kv, norm_final,
              ssm_log_dt, ssm_lambda_re, ssm_lambda_im, ssm_b_re, ssm_b_im, ssm_c_re, ssm_c_im, ssm_d,
              glu_w, glu_b, attn_w_q, attn_w_kv, attn_w_o):
    weights = (norm_ffn1, ffn1_w_in, ffn1_w_out, norm_mix, norm_ffn2, ffn2_w_in, ffn2_w_out, norm_kv, norm_final,
               ssm_log_dt, ssm_lambda_re, ssm_lambda_im, ssm_b_re, ssm_b_im, ssm_c_re, ssm_c_im, ssm_d,
               glu_w, glu_b, attn_w_q, attn_w_kv, attn_w_o)
    y_prompt, ssm_re_p, ssm_im_p, kv_p = _trunk(x_prompt, None, None, None, weights)
    y_sample, ssm_re_s, ssm_im_s, kv_s = _trunk(x_sample, state_ssm_re, state_ssm_im,
                                                 (cache_kv_w128, cache_kv_w512, cache_kv_w2048), weights)
    s_len = x_prompt.shape[1]
    kv128_p = kv_p[0][:, s_len - min(DIL_WINDOWS[0], s_len):]
    kv512_p = kv_p[1][:, s_len - min(DIL_WINDOWS[1], s_len):]
    kv2048_p = kv_p[2][:, s_len - min(DIL_WINDOWS[2], s_len):]
    return (y_prompt, y_sample, ssm_re_p, ssm_im_p, kv128_p, kv512_p, kv2048_p,
            ssm_re_s, ssm_im_s, kv_s[0], kv_s[1], kv_s[2])
```

```python
from contextlib import ExitStack
import math
import os
import numpy as np
import ml_dtypes
import concourse.bass as bass
import concourse.mybir as mybir
from concourse.bass_utils import run_bass_kernel_spmd

F32 = mybir.dt.float32
BF16 = mybir.dt.bfloat16
I32 = mybir.dt.int32
AF = mybir.ActivationFunctionType
ALU = mybir.AluOpType

SEM_LIMIT = 30000

D = 1024
SEQ = 2048
NS = 4
NT = SEQ + NS
DFF = 2816
NFF = DFF // 128
DEPTH = 4
NA = 2
EPS = 1e-6
CBS = [(0, 512), (512, 512), (1024, 512), (1536, 512), (2048, NS)]


class Op:
    __slots__ = ("eng", "fn", "deps", "dma_key", "dma_group", "idx", "signal", "ticket")

    def __init__(self, eng, fn, deps, dma_key, dma_group, idx):
        self.eng = eng
        self.fn = fn
        self.deps = deps
        self.dma_key = dma_key
        self.dma_group = dma_group
        self.idx = idx
        self.signal = False
        self.ticket = None


class Prog:
    ENGS = ("pe", "act", "dve", "pool", "sp")

    def __init__(self, nc):
        self.nc = nc
        self.ops = []
        self.last_w = {}
        self.readers = {}
        self.grp_ctr = 0
        self.last_eng = {}
        self.dma_since = []
        self.bar_deps = set()
        self.bar_pending = set()

    def barrier(self):
        self.bar_deps = set(self.last_eng.values()) | set(self.dma_since)
        self.bar_pending = set(self.ENGS)
        self.dma_since = []

    def op(self, eng, fn, reads=(), writes=(), dma_key=None, dma_group=None):
        deps = set()
        if eng in self.bar_pending:
            deps |= self.bar_deps
            self.bar_pending.discard(eng)
        for r in reads:
            w = self.last_w.get(r)
            if w is not None:
                deps.add(w)
        for w_ in writes:
            w = self.last_w.get(w_)
            if w is not None:
                deps.add(w)
            for rd in self.readers.get(w_, ()):
                deps.add(rd)
        idx = len(self.ops)
        if dma_key is not None:
            deps = {d for d in deps if not (self.ops[d].dma_key == dma_key and self.ops[d].dma_group == dma_group)}
        o = Op(eng, fn, deps, dma_key, dma_group, idx)
        self.ops.append(o)
        if dma_key is None:
            self.last_eng[eng] = idx
        else:
            self.dma_since.append(idx)
        for r in reads:
            self.readers.setdefault(r, []).append(idx)
        for w_ in writes:
            self.last_w[w_] = idx
            self.readers[w_] = []
        return idx

    def new_group(self):
        self.grp_ctr += 1
        return self.grp_ctr

    def dma(self, eng, out, in_, reads=(), writes=(), key=None, group=None, **kw):
        assert key is not None
        if group is None:
            group = self.new_group()
        return self.op(eng, lambda e: e.dma_start(out=out, in_=in_, **kw), reads, writes,
                       dma_key=key, dma_group=group)

    def emit(self):
        nc = self.nc
        ops = self.ops

        same_all = False

        def pe_pe(do, o):
            if do.dma_key is not None or o.dma_key is not None or do.eng != o.eng:
                return False
            return do.eng == "pe" or same_all

        for o in ops:
            for d in o.deps:
                if not pe_pe(ops[d], o):
                    ops[d].signal = True
        with ExitStack() as es:
            sems = {}

            def get_sem(name):
                if name not in sems:
                    sems[name] = es.enter_context(nc.semaphore(name))
                return sems[name]

            eng_cnt = {e: 0 for e in self.ENGS}
            key_cum, key_epoch, grp_size, grp_end = {}, {}, {}, {}
            for o in ops:
                if o.dma_key is not None:
                    g = (o.dma_key, o.dma_group)
                    grp_size[g] = grp_size.get(g, 0) + 1
            for o in ops:
                if o.dma_key is not None:
                    k = o.dma_key
                    g = (k, o.dma_group)
                    if g not in grp_end:
                        cum = key_cum.get(k, 0)
                        ep = key_epoch.get(k, 0)
                        if cum + 16 * grp_size[g] > SEM_LIMIT:
                            ep += 1
                            cum = 0
                        cum += 16 * grp_size[g]
                        key_cum[k] = cum
                        key_epoch[k] = ep
                        grp_end[g] = ("d_%s_%d" % (k, ep), cum)
                    o.ticket = grp_end[g]
                elif o.signal:
                    c = eng_cnt[o.eng]
                    eng_cnt[o.eng] = c + 1
                    o.ticket = ("e_%s_%d" % (o.eng, c // SEM_LIMIT), c % SEM_LIMIT + 1)
            last_tickets = {}
            for o in ops:
                if o.ticket is not None:
                    get_sem(o.ticket[0])
                    last_tickets[o.ticket[0]] = max(last_tickets.get(o.ticket[0], 0), o.ticket[1])
            per_eng = {e: [o for o in ops if o.eng == e] for e in self.ENGS}

            with nc.Block() as block:
                def run_engine(e, eh, final=False):
                    waited = {}
                    for o in per_eng[e]:
                        need = {}
                        for d in o.deps:
                            do = ops[d]
                            if do.ticket is None or pe_pe(do, o):
                                continue
                            s, v = do.ticket
                            if need.get(s, 0) < v:
                                need[s] = v
                        for s, v in need.items():
                            if waited.get(s, 0) >= v:
                                continue
                            eh.wait_ge(sems[s], v)
                            waited[s] = v
                        ins = o.fn(eh)
                        if o.dma_key is not None:
                            ins.then_inc(sems[o.ticket[0]], 16)
                        elif o.signal:
                            ins.then_inc(sems[o.ticket[0]], 1)
                    if final:
                        for s, v in last_tickets.items():
                            if waited.get(s, 0) < v:
                                eh.wait_ge(sems[s], v)

                @block.tensor
                def _(eh):
                    run_engine("pe", eh)

                @block.scalar
                def _(eh):
                    run_engine("act", eh)

                @block.vector
                def _(eh):
                    run_engine("dve", eh)

                @block.gpsimd
                def _(eh):
                    run_engine("pool", eh)

                @block.sync
                def _(eh):
                    run_engine("sp", eh, final=True)


def build(stage="full", debug=False):
    nc = bass.Bass("TRN2", target_bir_lowering=False)

    def din(name, shape, dt=F32):
        return nc.dram_tensor(name, list(shape), dt, kind="ExternalInput").ap()

    def dout(name, shape, dt=F32):
        return nc.dram_tensor(name, list(shape), dt, kind="ExternalOutput").ap()

    x_prompt = din("x_prompt", [SEQ, D])
    x_sample = din("x_sample", [NS, D])
    norm_ffn1 = din("norm_ffn1", [DEPTH, D])
    norm_mix = din("norm_mix", [DEPTH, D])
    norm_ffn2 = din("norm_ffn2", [DEPTH, D])
    norm_kv = din("norm_kv", [D])
    norm_final = din("norm_final", [D])
    ffn_w_in = [din("ffn1_w_in", [DEPTH, D, 2 * DFF]), din("ffn2_w_in", [DEPTH, D, 2 * DFF])]
    ffn_w_out = [din("ffn1_w_out", [DEPTH, DFF, D]), din("ffn2_w_out", [DEPTH, DFF, D])]
    state_re = din("state_ssm_re", [NA, NS, 64, 64])
    state_im = din("state_ssm_im", [NA, NS, 64, 64])
    ssm_log_dt = din("ssm_log_dt", [NA, 64])
    ssm_lre = din("ssm_lambda_re", [NA, 64, 64])
    ssm_lim = din("ssm_lambda_im", [NA, 64, 64])
    ssm_bre = din("ssm_b_re", [NA, 64, 64, 16])
    ssm_bim = din("ssm_b_im", [NA, 64, 64, 16])
    ssm_cre = din("ssm_c_re", [NA, 64, 16, 64])
    ssm_cim = din("ssm_c_im", [NA, 64, 16, 64])
    ssm_d = din("ssm_d", [NA, 64, 16])
    glu_w = din("glu_w", [NA, D, 2 * D])
    glu_b = din("glu_b", [NA, 2 * D])
    attn_w_q = din("attn_w_q", [2, D, 1536])
    attn_w_kv = din("attn_w_kv", [D, 3072])
    attn_w_o = din("attn_w_o", [2, 512, D])
    caches = [din("cache_kv_w128", [NS, 128, 2, 512]), din("cache_kv_w512", [NS, 512, 2, 512]), din("cache_kv_w2048", [NS, 2048, 2, 512])]
    c_bias = din("c_bias", [3, 2, 2, 128, 512])
    c_biasc = din("c_biasc", [3, 2, 128, 4])
    kvd = nc.dram_tensor("kvd", [3, 2, NT, 512], F32, kind="Internal").ap()
    ktd = nc.dram_tensor("ktd", [3, 4, 128, NT], BF16, kind="Internal").ap()
    o_kv128_p = dout("kv_w128_prompt", [128, 2, 512])
    o_kv512_p = dout("kv_w512_prompt", [512, 2, 512])
    o_kv2048_p = dout("kv_w2048_prompt", [2048, 2, 512])
    o_kv128_s = dout("kv_w128_sample", [NS, 2, 512])
    o_kv512_s = dout("kv_w512_sample", [NS, 2, 512])
    o_kv2048_s = dout("kv_w2048_sample", [NS, 2, 512])
    c_ident = din("c_ident", [128, 128])
    c_tri = din("c_tri", [128, 256])
    c_maskg = din("c_maskg", [128, 8])
    y_prompt = dout("y_prompt", [SEQ, D])
    y_sample = dout("y_sample", [NS, D])
    o_ssm_re_p = dout("ssm_re_prompt", [NA, 64, 64])
    o_ssm_im_p = dout("ssm_im_prompt", [NA, 64, 64])
    o_ssm_re_s = dout("ssm_re_sample", [NA, NS, 64, 64])
    o_ssm_im_s = dout("ssm_im_sample", [NA, NS, 64, 64])
    dbg = dout("dbg", [128, 8, NT]) if debug else None

    with ExitStack() as es:
        uniq = [0]

        def sbt(stack, name, shape, dt):
            uniq[0] += 1
            return stack.enter_context(nc.sbuf_tensor("%s_%d" % (name, uniq[0]), list(shape), dt))

        p = Prog(nc)
        es.enter_context(nc.allow_non_contiguous_dma(reason="small strided parameter loads"))

        resid = sbt(es, "resid", [128, 8, NT], F32)
        xn = sbt(es, "xn", [128, 8, NT], BF16)
        ident = sbt(es, "ident", [128, 128], F32)
        ones_bf = sbt(es, "ones_bf", [128, 128], BF16)
        gains = sbt(es, "gains", [128, 8, 16], F32)
        carryP = sbt(es, "carryP", [128, 32, 2], F32)
        PS = [es.enter_context(nc.psum_tensor("ps%d" % i, [128, 512], F32)) for i in range(8)]
        B = {}

        def T(o_, i_, idn):
            return lambda e: e.transpose(out=o_, in_=i_, identity=idn)

        def MM(o_, l_, r_, st, sp_):
            return lambda e: e.matmul(o_, lhsT=l_, rhs=r_, start=st, stop=sp_)

        def CP(o_, i_):
            return lambda e: e.tensor_copy(out=o_, in_=i_)

        def ACT(o_, i_, f, **kw):
            return lambda e: e.activation(out=o_, in_=i_, func=f, **kw)

        def TT(o_, a_, b_, op):
            return lambda e: e.tensor_tensor(out=o_, in0=a_, in1=b_, op=op)

        def TS(o_, a_, s1, s2, op0, op1=None):
            if op1 is None:
                return lambda e: e.tensor_scalar(out=o_, in0=a_, scalar1=s1, scalar2=None, op0=op0)
            return lambda e: e.tensor_scalar(out=o_, in0=a_, scalar1=s1, scalar2=s2, op0=op0, op1=op1)

        def STT(o_, a_, sc, b_, op0, op1):
            return lambda e: e.scalar_tensor_tensor(out=o_, in0=a_, scalar=sc, in1=b_, op0=op0, op1=op1)

        def alloc_norm(st):
            B["sq"] = [sbt(st, "sq%d" % i, [128, 8, 512], BF16) for i in range(2)]
            B["lnt"] = sbt(st, "lnt", [128, 512], F32)
            B["rstd"] = [sbt(st, "rstd%d" % i, [128, 512], F32) for i in range(2)]

        def alloc_w(st):
            B["W8"] = [sbt(st, "w8_%d" % i, [128, 8, 512], BF16) for i in range(4)]
            B["WO"] = [sbt(st, "wo_%d" % i, [128, 4, 1024], BF16) for i in range(2)]
            B["hbuf"] = sbt(st, "hbuf", [128, 4, NT], BF16)
            B["sA"] = [sbt(st, "sA%d" % i, [128, 512], F32) for i in range(2)]

        ALLRES = [("resid", f, c) for f in range(8) for c in range(5)]

        p.dma("sp", ident[:], c_ident, writes=["ident"], key="init")
        p.op("dve", lambda e: e.memset(ones_bf[:], 1.0), writes=["ones"])
        p.op("dve", lambda e: e.memset(carryP[:], 0.0), writes=["carryP"])
        gst = ExitStack()
        gnat = sbt(gst, "gnat", [16, 1024], F32)
        g0 = p.new_group()
        p.dma("sp", gnat[0:4, :], norm_ffn1, writes=["gnat"], key="init2", group=g0)
        p.dma("sp", gnat[4:8, :], norm_mix, writes=["gnat"], key="init2", group=g0)
        p.dma("sp", gnat[8:12, :], norm_ffn2, writes=["gnat"], key="init2", group=g0)
        p.dma("sp", gnat[12:13, :], norm_kv.rearrange("(a f) -> a f", a=1), writes=["gnat"], key="init2", group=g0)
        p.dma("sp", gnat[13:14, :], norm_final.rearrange("(a f) -> a f", a=1), writes=["gnat"], key="init2", group=g0)
        for ft in range(8):
            p.op("pe", T(PS[6][:, ft * 16:ft * 16 + 14], gnat[0:14, ft * 128:(ft + 1) * 128], ident[0:14, 0:14]),
                 reads=["gnat", "ident"], writes=[("ps", 6)])
        p.op("dve", CP(gains[:, :, 0:14], PS[6][:, 0:128].rearrange("p (a b) -> p a b", a=8)[:, :, 0:14]),
             reads=[("ps", 6)], writes=["gains"])
        p.barrier()
        gst.close()

        with ExitStack() as ph:
            xtok = [sbt(ph, "xtok%d" % i, [128, 1024], F32) for i in range(4)]
            for tt in range(16):
                s = tt % 4
                p.dma("sp", xtok[s][:], x_prompt[tt * 128:(tt + 1) * 128, :], writes=[("xtok", s)], key="xtok%d" % s)
                for half in range(2):
                    bank = PS[6 + half]
                    for f4 in range(4):
                        ft = half * 4 + f4
                        p.op("pe", T(bank[:, f4 * 128:(f4 + 1) * 128], xtok[s][:, ft * 128:(ft + 1) * 128], ident[:]),
                             reads=[("xtok", s), "ident"], writes=[("ps", 6 + half)])
                    dst_ = resid[:, half * 4:half * 4 + 4, tt * 128:(tt + 1) * 128]
                    src_ = bank[:, :].rearrange("p (a b) -> p a b", a=4)
                    p.op("dve" if half == 0 else "act", CP(dst_, src_) if half == 0 else ACT(dst_, src_, AF.Copy),
                         reads=[("ps", 6 + half)], writes=[("resid", half * 4 + f, tt // 4) for f in range(4)])
            p.dma("sp", xtok[0][0:NS, :], x_sample, writes=[("xtok", 0)], key="xtok0")
            for ft in range(8):
                p.op("pe", T(PS[6][:, ft * NS:(ft + 1) * NS], xtok[0][0:NS, ft * 128:(ft + 1) * 128], ident[0:NS, 0:NS]),
                     reads=[("xtok", 0), "ident"], writes=[("ps", 6)])
            p.op("dve", CP(resid[:, :, SEQ:NT], PS[6][:, 0:8 * NS].rearrange("p (a b) -> p a b", a=8)),
                 reads=[("ps", 6)], writes=[("resid", f, 4) for f in range(8)])
            p.barrier()

        norm_ctr = [0]

        def rmsnorm(gidx, out_f32=False):
            sq, lnt, rstd = B["sq"], B["lnt"], B["rstd"]
            for ci, (c0, n) in enumerate(CBS):
                k = norm_ctr[0]
                norm_ctr[0] += 1
                s = k % 2
                bank = PS[6 + s]
                p.op("act", ACT(sq[s][:, :, 0:n], resid[:, :, c0:c0 + n], AF.Square),
                     reads=[("resid", f, ci) for f in range(8)], writes=[("sq", s)])
                for kt in range(8):
                    p.op("pe", MM(bank[:, 0:n], ones_bf[:], sq[s][:, kt, 0:n], kt == 0, kt == 7),
                         reads=[("sq", s), "ones"], writes=[("ps", 6 + s)])
                p.op("act", ACT(lnt[:, 0:n], bank[:, 0:n], AF.Ln, scale=1.0 / D, bias=EPS),
                     reads=[("ps", 6 + s)], writes=["lnt"])
                p.op("act", ACT(rstd[s][:, 0:n], lnt[:, 0:n], AF.Exp, scale=-0.5), reads=["lnt"], writes=[("rstd", s)])
                for ft in range(8):
                    dst = resid[:, ft, c0:c0 + n] if out_f32 else xn[:, ft, c0:c0 + n]
                    wkey = ("resid", ft, ci) if out_f32 else ("xn", ft, ci)
                    p.op("dve", STT(dst, resid[:, ft, c0:c0 + n], gains[:, ft, gidx:gidx + 1], rstd[s][:, 0:n], ALU.mult, ALU.mult),
                         reads=[("resid", ft, ci), ("rstd", s), "gains"], writes=[wkey])

        w8_ctr = [0]
        wo_ctr = [0]
        unit_ctr = [0]
        dn_ctr = [0]

        def load_w8(src_ap):
            s = w8_ctr[0] % len(B["W8"])
            w8_ctr[0] += 1
            ncols = src_ap.shape[1]
            p.dma("pool", B["W8"][s][:, :, 0:ncols], src_ap.rearrange("(k p) c -> p k c", p=128),
                  writes=[("w8", s)], key="w8_%d" % s)
            return s

        def load_wo(src_ap):
            s = wo_ctr[0] % len(B["WO"])
            wo_ctr[0] += 1
            n = src_ap.shape[0] // 128
            p.dma("pool", B["WO"][s][:, 0:n, :], src_ap.rearrange("(k p) c -> p k c", p=128),
                  writes=[("wo", s)], key="wo_%d" % s)
            return s

        def ffn(layer, which, gidx):
            W8, WO, hbuf, sA = B["W8"], B["WO"], B["hbuf"], B["sA"]
            w_in = ffn_w_in[which][layer]
            w_out = ffn_w_out[which][layer]
            rmsnorm(gidx)
            parts = [(t0, min(4, NFF - t0)) for t0 in range(0, NFF, 4)]
            CBS_F = CBS
            for (t0, n) in parts:
                sa_ = load_w8(w_in[:, t0 * 128:(t0 + n) * 128])
                sb_ = load_w8(w_in[:, DFF + t0 * 128:DFF + (t0 + n) * 128])
                so_ = load_wo(w_out[t0 * 128:(t0 + n) * 128, :])
                for j in range(n):
                    for ci, (c0, nn) in enumerate(CBS_F):
                        u = unit_ctr[0]
                        unit_ctr[0] += 1
                        ba, bb = u % 2, 2 + u % 2
                        for kt in range(8):
                            p.op("pe", MM(PS[ba][:, 0:nn], W8[sa_][:, kt, j * 128:(j + 1) * 128], xn[:, kt, c0:c0 + nn], kt == 0, kt == 7),
                                 reads=[("w8", sa_), ("xn", kt, ci)], writes=[("ps", ba)])
                        for kt in range(8):
                            p.op("pe", MM(PS[bb][:, 0:nn], W8[sb_][:, kt, j * 128:(j + 1) * 128], xn[:, kt, c0:c0 + nn], kt == 0, kt == 7),
                                 reads=[("w8", sb_), ("xn", kt, ci)], writes=[("ps", bb)])
                        p.op("act", ACT(sA[u % 2][:, 0:nn], PS[ba][:, 0:nn], AF.Silu), reads=[("ps", ba)], writes=[("sA", u % 2)])
                        p.op("dve", TT(hbuf[:, j, c0:c0 + nn], PS[bb][:, 0:nn], sA[u % 2][:, 0:nn], ALU.mult),
                             reads=[("ps", bb), ("sA", u % 2)], writes=[("h", j, ci)])
                for fo in range(8):
                    for ci, (c0, nn) in enumerate(CBS_F):
                        dsel = 4 + dn_ctr[0] % 2
                        dn_ctr[0] += 1
                        for j in range(n):
                            p.op("pe", MM(PS[dsel][:, 0:nn], WO[so_][:, j, fo * 128:(fo + 1) * 128], hbuf[:, j, c0:c0 + nn], j == 0, j == n - 1),
                                 reads=[("wo", so_), ("h", j, ci)], writes=[("ps", dsel)])
                        p.op("dve", STT(resid[:, fo, c0:c0 + nn], PS[dsel][:, 0:nn], 0.5, resid[:, fo, c0:c0 + nn], ALU.mult, ALU.add),
                             reads=[("ps", dsel), ("resid", fo, ci)], writes=[("resid", fo, ci)])

        def ffn_phase(layer, which, gidx):
            with ExitStack() as ph:
                alloc_norm(ph)
                alloc_w(ph)
                ffn(layer, which, gidx)
                p.barrier()

        PI = math.pi
        seqs = [("P", c) for c in range(16)] + [("S", b) for b in range(NS)]

        def s5_layer(l):
            with ExitStack() as ph:
                alloc_norm(ph)
                rmsnorm(4 + l)
                p.barrier()
            with ExitStack() as ph:
                sm = lambda name, shape, dt=F32: sbt(ph, "s5_" + name, shape, dt)
                nat = sm("nat", [32, 128])
                tri_bf = sm("tri_bf", [128, 256], BF16)
                maskg = sm("maskg", [128, 8])
                p.dma("pool", tri_bf[:], c_tri, writes=["tri"], key="initp")
                p.dma("sp", maskg[:], c_maskg, writes=["maskg"], key="init3")
                ldt = sm("ldt", [32, 2])
                lr = sm("lr", [128, 32]); li = sm("li", [128, 32]); dt_ = sm("dt", [128, 32])
                t1 = sm("t1", [128, 32]); t2 = sm("t2", [128, 32]); t3 = sm("t3", [128, 32])
                mag = sm("mag", [128, 32]); imag = sm("imag", [128, 32])
                sn = sm("sn", [128, 32]); cs = sm("cs", [128, 32])
                ab = sm("ab", [128, 32, 2]); abi = sm("abi", [128, 32, 2])
                czr = sm("czr", [128, 32]); czi = sm("czi", [128, 32])
                dsk = sm("dsk", [128, 8])
                Ep = sm("Ep", [128, 16, 2, 128]); EmT = sm("EmT", [128, 16, 2, 128])
                Bw4 = sm("Bw4", [128, 16, 512], BF16)
                Cw = sm("Cw", [128, 16, 3, 128], BF16)
                XA = sm("XA", [128, 16, 128])
                d1 = sm("d1", [128, 16, 64]); d2 = sm("d2", [128, 16, 64])
                d3 = sm("d3", [128, 16, 64]); d4 = sm("d4", [128, 16, 64])
                bnr = sm("bnr", [128, 16, 16]); bni = sm("bni", [128, 16, 16])
                bbr = sm("bbr", [128, 16, 16]); bbi = sm("bbi", [128, 16, 16]); bt = sm("bt", [128, 16, 16])
                cnr = sm("cnr", [128, 4, 64]); cni = sm("cni", [128, 4, 64])
                qb = [sm("q%d" % i, [128, 512], BF16) for i in range(3)]
                wb = [sm("w%d" % i, [128, 4, 128], BF16) for i in range(3)]
                wS = [sm("wS%d" % i, [128, 2, NS], BF16) for i in range(3)]
                SS = sm("SS", [128, 32, 2, NS]); CPs = sm("CPs", [128, 32, 2, NS]); NSt = sm("NSt", [128, 32, 2, NS])
                sc1 = sm("sc1", [128, 32, NS]); sc2 = sm("sc2", [128, 32, NS])
                cab4 = sm("cab4", [128, 4, 2]); ct4 = sm("ct4", [128, 4, 2]); cu4 = sm("cu4", [128, 4, 2])
                yv = [sm("yv%d" % i, [128, 128]) for i in range(2)]
                cab = sm("cab", [128, 2]); ct = sm("ct", [128, 2]); cu = sm("cu", [128, 2])
                sout = sm("sout", [32, 128])

                natr = [sm("natr%d" % i, [32, 128]) for i in range(4)]
                soutr = [sm("soutr%d" % i, [32, 128]) for i in range(4)]
                rr = [0, 0]

                def load_T(dst, src_nat_ap, key):
                    i_ = rr[0] % 4
                    rr[0] += 1
                    bk_ = 6 + i_ % 2
                    p.dma("sp", natr[i_][:], src_nat_ap, writes=[("natr", i_)], key="s5natr%d" % i_)
                    p.op("pe", T(PS[bk_][:, 0:32], natr[i_][:], ident[0:32, 0:32]), reads=[("natr", i_), "ident"], writes=[("ps", bk_)])
                    p.op("dve", CP(dst, PS[bk_][:, 0:32]), reads=[("ps", bk_)], writes=[key])

                load_T(lr[:], ssm_lre[l].rearrange("(b g) p -> b (g p)", g=2), "lr")
                load_T(li[:], ssm_lim[l].rearrange("(b g) p -> b (g p)", g=2), "li")
                p.dma("sp", ldt[:], ssm_log_dt[l].rearrange("(b g) -> b g", g=2), writes=["ldt"], key="s5ldt")
                p.op("dve", CP(nat[:].rearrange("b (g p) -> b g p", g=2), ldt[:].unsqueeze(2).to_broadcast([32, 2, 64])),
                     reads=["ldt"], writes=["nat"])
                p.op("pe", T(PS[7][:, 0:32], nat[:], ident[0:32, 0:32]), reads=["nat", "ident"], writes=[("ps", 7)])
                p.op("act", ACT(dt_[:], PS[7][:, 0:32], AF.Exp), reads=[("ps", 7)], writes=["dt"])
                for b in range(NS):
                    load_T(SS[:, :, 0, b], state_re[l, b].rearrange("(b g) p -> b (g p)", g=2), "SS")
                    load_T(SS[:, :, 1, b], state_im[l, b].rearrange("(b g) p -> b (g p)", g=2), "SS")
                p.dma("sp", nat[0:8, :], ssm_d[l].rearrange("(f g) h -> f (g h)", g=8), writes=["nat"], key="s5nat")
                p.op("pe", T(PS[7][:, 0:8], nat[0:8, :], ident[0:8, 0:8]), reads=["nat", "ident"], writes=[("ps", 7)])
                p.op("dve", CP(dsk[:], PS[7][:, 0:8]), reads=[("ps", 7)], writes=["dsk"])

                V = lambda f, reads, writes: p.op("dve", f, reads=reads, writes=writes)
                V(TT(t1[:], lr[:], dt_[:], ALU.mult), ["lr", "dt"], ["t1"])
                for (dst, sgn, key) in ((mag, 1.0, "mag"), (imag, -1.0, "imag")):
                    V(TS(t2[:], t1[:], sgn / 6.0, 1.0, ALU.mult, ALU.add), ["t1"], ["t2"])
                    for kk in (5.0, 4.0, 3.0, 2.0, 1.0):
                        V(TT(t2[:], t2[:], t1[:], ALU.mult), ["t1", "t2"], ["t2"])
                        V(TS(t2[:], t2[:], sgn / kk, 1.0, ALU.mult, ALU.add), ["t2"], ["t2"])
                    V(CP(dst[:], t2[:]), ["t2"], [key])
                V(TT(t1[:], li[:], dt_[:], ALU.mult), ["li", "dt"], ["t1"])
                for _ in range(5):
                    V(TS(t2[:], t1[:], PI, 2.0 * PI, ALU.is_gt, ALU.mult), ["t1"], ["t2"])
                    V(TT(t1[:], t1[:], t2[:], ALU.subtract), ["t1", "t2"], ["t1"])
                p.op("act", ACT(sn[:], t1[:], AF.Sin), reads=["t1"], writes=["sn"])
                V(TS(t3[:], t1[:], PI / 2.0, None, ALU.add), ["t1"], ["t3"])
                V(TS(t2[:], t3[:], PI, 2.0 * PI, ALU.is_gt, ALU.mult), ["t3"], ["t2"])
                V(TT(t3[:], t3[:], t2[:], ALU.subtract), ["t3", "t2"], ["t3"])
                p.op("act", ACT(cs[:], t3[:], AF.Sin), reads=["t3"], writes=["cs"])
                V(TT(ab[:, :, 0], mag[:], cs[:], ALU.mult), ["mag", "cs"], ["ab"])
                V(TT(ab[:, :, 1], mag[:], sn[:], ALU.mult), ["mag", "sn"], ["ab"])
                V(TT(abi[:, :, 0], imag[:], cs[:], ALU.mult), ["imag", "cs"], ["abi"])
                V(TT(t2[:], imag[:], sn[:], ALU.mult), ["imag", "sn"], ["t2"])
                V(TS(abi[:, :, 1], t2[:], -1.0, None, ALU.mult), ["t2"], ["abi"])
                V(TT(t1[:], lr[:], lr[:], ALU.mult), ["lr"], ["t1"])
                V(TT(t2[:], li[:], li[:], ALU.mult), ["li"], ["t2"])
                V(TT(t1[:], t1[:], t2[:], ALU.add), ["t1", "t2"], ["t1"])
                V(lambda e: e.reciprocal(out=t1[:], in_=t1[:]), ["t1"], ["t1"])
                V(TS(t3[:], ab[:, :, 0], -1.0, None, ALU.add), ["ab"], ["t3"])
                V(TT(t2[:], t3[:], lr[:], ALU.mult), ["t3", "lr"], ["t2"])
                V(TT(czr[:], ab[:, :, 1], li[:], ALU.mult), ["ab", "li"], ["czr"])
                V(TT(czr[:], czr[:], t2[:], ALU.add), ["czr", "t2"], ["czr"])
                V(TT(czr[:], czr[:], t1[:], ALU.mult), ["czr", "t1"], ["czr"])
                V(TT(t2[:], ab[:, :, 1], lr[:], ALU.mult), ["ab", "lr"], ["t2"])
                V(TT(czi[:], t3[:], li[:], ALU.mult), ["t3", "li"], ["czi"])
                V(TT(czi[:], t2[:], czi[:], ALU.subtract), ["czi", "t2"], ["czi"])
                V(TT(czi[:], czi[:], t1[:], ALU.mult), ["czi", "t1"], ["czi"])

                def powers(dst, base, hb):
                    bs = base[:, hb * 16:(hb + 1) * 16, :]
                    V(CP(dst[:, :, :, 0], bs), ["ab", "abi"], ["E"])
                    m = 1
                    while m < 128:
                        pr = dst[:, :, 0, m - 1:m].to_broadcast([128, 16, m])
                        pi_ = dst[:, :, 1, m - 1:m].to_broadcast([128, 16, m])
                        er = dst[:, :, 0, 0:m]
                        ei = dst[:, :, 1, 0:m]
                        a1 = d1[:, :, 0:m]
                        a2 = d2[:, :, 0:m]
                        a3 = d3[:, :, 0:m]
                        a4 = d4[:, :, 0:m]
                        V(TT(a1, er, pr, ALU.mult), ["E"], ["d1"])
                        V(TT(a2, ei, pi_, ALU.mult), ["E"], ["d2"])
                        V(TT(a3, er, pi_, ALU.mult), ["E"], ["d3"])
                        V(TT(a4, ei, pr, ALU.mult), ["E"], ["d4"])
                        V(TT(dst[:, :, 0, m:2 * m], a1, a2, ALU.subtract), ["d1", "d2"], ["Er"])
                        V(TT(dst[:, :, 1, m:2 * m], a3, a4, ALU.add), ["d3", "d4"], ["Ei"])
                        p.last_w["E"] = p.last_w["Ei"]
                        m *= 2

                for hb in range(2):
                    powers(Ep, abi, hb)
                    for lb in range(16):
                        for pl in range(2):
                            bank = PS[6 + (lb // 2) % 2]
                            slot = (lb % 2) * 2 + pl
                            p.op("pe", T(bank[:, slot * 128:(slot + 1) * 128], Ep[:, lb, pl, :], ident[:]),
                                 reads=["E", "ident"], writes=[("ps", 6 + (lb // 2) % 2)])
                        if lb % 2 == 1:
                            p.op("dve", CP(EmT[:, lb - 1:lb + 1, :, :], bank[:, :].rearrange("p (a b c) -> p a b c", a=2, b=2)),
                                 reads=[("ps", 6 + (lb // 2) % 2)], writes=["EmT"])
                    powers(Ep, ab, hb)
                    gb = p.new_group()
                    for (dst, src) in ((bnr, ssm_bre), (bni, ssm_bim)):
                        for q4 in range(4):
                            p.dma("sp", dst[:, q4 * 4:(q4 + 1) * 4, :],
                                  src[l, hb * 32 + q4 * 8:hb * 32 + q4 * 8 + 8].rearrange("(b g) p h -> (g p) b h", g=2),
                                  writes=["bn"], key="s5b", group=gb)
                    zr = czr[:, hb * 16:(hb + 1) * 16].unsqueeze(2).to_broadcast([128, 16, 16])
                    zi = czi[:, hb * 16:(hb + 1) * 16].unsqueeze(2).to_broadcast([128, 16, 16])
                    V(TT(bbr[:], bnr[:], zr, ALU.mult), ["bn", "czr"], ["bbr"])
                    V(TT(bt[:], bni[:], zi, ALU.mult), ["bn", "czi"], ["bt"])
                    V(TT(bbr[:], bbr[:], bt[:], ALU.subtract), ["bbr", "bt"], ["bbr"])
                    V(TT(bbi[:], bni[:], zr, ALU.mult), ["bn", "czr"], ["bbi"])
                    V(TT(bt[:], bnr[:], zi, ALU.mult), ["bn", "czi"], ["bt"])
                    V(TT(bbi[:], bbi[:], bt[:], ALU.add), ["bbi", "bt"], ["bbi"])
                    for (pl, src, planes) in ((0, bbr, (0, 3)), (1, bbi, (1, 2))):
                        p.op("pool", lambda e: e.memset(XA[:], 0.0), reads=[], writes=["XA"])
                        for g2 in range(2):
                            for k in range(4):
                                V(CP(XA[g2 * 64:(g2 + 1) * 64, k::4, 32 * k + 16 * g2:32 * k + 16 * g2 + 16],
                                     src[g2 * 64:(g2 + 1) * 64, k::4, :]), ["bbr", "bbi"], ["XA"])
                        for lb in range(16):
                            bsel = 6 + (lb // 4) % 2
                            p.op("pe", T(PS[bsel][:, (lb % 4) * 128:(lb % 4 + 1) * 128], XA[:, lb, :], ident[:]),
                                 reads=["XA", "ident"], writes=[("ps", bsel)])
                            if lb % 4 == 3:
                                for pln in planes:
                                    p.op("act", ACT(Bw4[:, lb - 3:lb + 1, pln * 128:(pln + 1) * 128],
                                                    PS[bsel][:, :].rearrange("p (a b) -> p a b", a=4), AF.Copy),
                                         reads=[("ps", bsel)], writes=["Bw4"])
                    gc_ = p.new_group()
                    for (dst, src) in ((cnr, ssm_cre), (cni, ssm_cim)):
                        p.dma("sp", dst[:], src[l, hb * 32:(hb + 1) * 32].rearrange("(f g) h p -> (g h) f p", g=8),
                              writes=["cn"], key="s5c", group=gc_)
                    for (src, slots) in ((cnr, ((0, 1.0), (1, -1.0))), (cni, ((2, -1.0),))):
                        for g2 in range(2):
                            for k in range(4):
                                V(TS(XA[:, k::4, g2 * 64:(g2 + 1) * 64], src[:], maskg[:, 2 * k + g2:2 * k + g2 + 1], None, ALU.mult),
                                  ["cn", "maskg"], ["XA"])
                        for lb in range(16):
                            bsel = 6 + (lb // 4) % 2
                            p.op("pe", T(PS[bsel][:, (lb % 4) * 128:(lb % 4 + 1) * 128], XA[:, lb, :], ident[:]),
                                 reads=["XA", "ident"], writes=[("ps", bsel)])
                            if lb % 4 == 3:
                                for (sl, sc) in slots:
                                    p.op("act", ACT(Cw[:, lb - 3:lb + 1, sl, :],
                                                    PS[bsel][:, :].rearrange("p (a b) -> p a b", a=4), AF.Copy, scale=sc),
                                         reads=[("ps", bsel)], writes=["Cw"])
                    p.barrier()
                    if hb == 0:
                        arb = ab[:, :, 0:1].to_broadcast([128, 32, NS])
                        aib = ab[:, :, 1:2].to_broadcast([128, 32, NS])
                        V(TT(sc1[:], SS[:, :, 0, :], arb, ALU.mult), ["SS", "ab"], ["sc1"])
                        V(TT(sc2[:], SS[:, :, 1, :], aib, ALU.mult), ["SS", "ab"], ["sc2"])
                        V(TT(CPs[:, :, 0, :], sc1[:], sc2[:], ALU.subtract), ["sc1", "sc2"], ["CPs"])
                        V(TT(sc1[:], SS[:, :, 1, :], arb, ALU.mult), ["SS", "ab"], ["sc1"])
                        V(TT(sc2[:], SS[:, :, 0, :], aib, ALU.mult), ["SS", "ab"], ["sc2"])
                        V(TT(CPs[:, :, 1, :], sc1[:], sc2[:], ALU.add), ["sc1", "sc2"], ["CPs"])
                    BB = (0, 1, 6)
                    PB = (2, 3, 7)
                    QPOOL = False
                    bqs = [d1[:, 0:8, :].rearrange("p a b -> p (a b)"), d1[:, 8:16, :].rearrange("p a b -> p (a b)"),
                           d2[:, 0:8, :].rearrange("p a b -> p (a b)")]
                    iters = []
                    gi = 0
                    for f4 in range(4):
                        ft = hb * 4 + f4
                        for si in range(17):
                            for k in range(4):
                                iters.append((f4, ft, si, k, gi, len(iters)))
                            gi += 1

                    def st1(itd):
                        f4, ft, si, k, gi, i = itd
                        lb = f4 * 4 + k
                        bk = BB[i % 3]
                        if si < 16:
                            c0 = si * 128
                            p.op("pe", MM(PS[bk][:, :], xn[:, ft, c0:c0 + 128], Bw4[:, lb, :], True, True),
                                 reads=[("xn", ft, si // 4), "Bw4"], writes=[("ps", bk)])
                        else:
                            for pl in range(2):
                                p.op("pe", MM(PS[bk][:, pl * NS:(pl + 1) * NS], Bw4[:, lb, pl * 128:(pl + 1) * 128], xn[:, ft, SEQ:NT], True, True),
                                     reads=[("xn", ft, 4), "Bw4"], writes=[("ps", bk)])

                    def st2(itd):
                        f4, ft, si, k, gi, i = itd
                        lb = f4 * 4 + k
                        blk = hb * 16 + lb
                        bk = BB[i % 3]
                        if si < 16:
                            q = qb[i % 3]
                            if QPOOL and i % 3 == 2:
                                bq = bqs[i % 3]
                                p.op("act", ACT(bq, PS[bk][:, :], AF.Copy), reads=[("ps", bk)], writes=[("bq", i % 3)])
                                p.op("pool", TT(q[:, :].rearrange("p (a b) -> p a b", a=2), bq.rearrange("p (a b) -> p a b", a=2),
                                                EmT[:, lb:lb + 1, :, :].rearrange("p a b c -> p a (b c)").to_broadcast([128, 2, 256]), ALU.mult),
                                     reads=[("bq", i % 3), "EmT"], writes=[("q", i % 3)])
                            else:
                                V(TT(q[:, :].rearrange("p (a b) -> p a b", a=2), PS[bk][:, :].rearrange("p (a b) -> p a b", a=2),
                                     EmT[:, lb:lb + 1, :, :].rearrange("p a b c -> p a (b c)").to_broadcast([128, 2, 256]), ALU.mult),
                                  [("ps", bk), "EmT"], [("q", i % 3)])
                        else:
                            V(TT(NSt[:, blk, :, :], PS[bk][:, 0:2 * NS].rearrange("p (a b) -> p a b", a=2), CPs[:, blk, :, :], ALU.add),
                              [("ps", bk), "CPs"], [("NSt", blk)])
                            V(CP(wS[i % 3][:], NSt[:, blk, :, :]), [("NSt", blk)], [("wS", i % 3)])

                    def st3(itd):
                        f4, ft, si, k, gi, i = itd
                        if si >= 16:
                            return
                        q = qb[i % 3]
                        PP = PS[PB[i % 3]]
                        rk, wk = [("q", i % 3), "tri"], [("ps", PB[i % 3])]
                        p.op("pe", MM(PP[:, 0:128], q[:, 0:128], tri_bf[:, 0:128], True, False), reads=rk, writes=wk)
                        p.op("pe", MM(PP[:, 0:128], q[:, 128:256], tri_bf[:, 128:256], False, True), reads=rk, writes=wk)
                        p.op("pe", MM(PP[:, 128:256], q[:, 256:384], tri_bf[:, 0:128], True, False), reads=rk, writes=wk)
                        p.op("pe", MM(PP[:, 128:256], q[:, 384:512], tri_bf[:, 0:128], False, True), reads=rk, writes=wk)

                    def st4(itd):
                        f4, ft, si, k, gi, i = itd
                        if si >= 16:
                            return
                        lb = f4 * 4 + k
                        blk = hb * 16 + lb
                        PP = PS[PB[i % 3]]
                        pk = ("ps", PB[i % 3])
                        w = wb[i % 3]
                        ck = ("carryP", blk)
                        cki = ("carryPi", blk)
                        V(STT(w[:, 0:2, :], PP[:, 0:128].unsqueeze(1).to_broadcast([128, 2, 128]), carryP[:, blk, 0:1],
                              Ep[:, lb, :, :], ALU.add, ALU.mult), [pk, "E", ck], [("w", i % 3)])
                        V(STT(w[:, 2:4, :], PP[:, 128:256].unsqueeze(1).to_broadcast([128, 2, 128]), carryP[:, blk, 1:2],
                              Ep[:, lb, :, :], ALU.add, ALU.mult), [pk, "E", cki], [("w2", i % 3)])
                        V(TT(cab4[:, k, :], PP[:, 127:256:128], carryP[:, blk, :], ALU.add), [pk, ck, cki], [("cab4", k)])
                        if k == 3:
                            lb0 = f4 * 4
                            blk0 = hb * 16 + lb0
                            E4 = Ep[:, lb0:lb0 + 4, :, 127]
                            cks = [("carryP", blk0 + kk) for kk in range(4)]
                            c4 = [("cab4", kk) for kk in range(4)]
                            V(TT(ct4[:], E4, cab4[:, :, 0:1].to_broadcast([128, 4, 2]), ALU.mult), c4 + ["E"], ["ct4"])
                            V(TT(cu4[:], E4, cab4[:, :, 1:2].to_broadcast([128, 4, 2]), ALU.mult), c4 + ["E"], ["cu4"])
                            V(TT(carryP[:, blk0:blk0 + 4, 0], ct4[:, :, 0], cu4[:, :, 1], ALU.subtract), ["ct4", "cu4"], cks)
                            V(TT(carryP[:, blk0:blk0 + 4, 1], ct4[:, :, 1], cu4[:, :, 0], ALU.add), ["ct4", "cu4"], [("carryPi", blk0 + kk) for kk in range(4)])

                    def st5(itd):
                        f4, ft, si, k, gi, i = itd
                        lb = f4 * 4 + k
                        ybank = 4 + gi % 2
                        if si < 16:
                            c0, L, ci = si * 128, 128, si // 4
                            w = wb[i % 3]
                            for wi, (pl, sl) in enumerate(((0, 0), (3, 1), (2, 2), (1, 2))):
                                p.op("pe", MM(PS[ybank][:, 0:L], Cw[:, lb, sl, :], w[:, pl, :], k == 0 and wi == 0, k == 3 and wi == 3),
                                     reads=[("w", i % 3), ("w2", i % 3), "Cw"], writes=[("ps", ybank)])
                        else:
                            c0, L, ci = SEQ, NS, 4
                            for wi, (pl, sl) in enumerate(((0, 0), (1, 2))):
                                p.op("pe", MM(PS[ybank][:, 0:L], Cw[:, lb, sl, :], wS[i % 3][:, pl, :], k == 0 and wi == 0, k == 3 and wi == 1),
                                     reads=[("wS", i % 3), "Cw"], writes=[("ps", ybank)])
                        if k == 3:
                            yy = yv[gi % 2]
                            V(STT(yy[:, 0:L], xn[:, ft, c0:c0 + L], dsk[:, ft:ft + 1], PS[ybank][:, 0:L], ALU.mult, ALU.add),
                              [("ps", ybank), ("xn", ft, ci), "dsk"], [("yv", gi % 2)])
                            p.op("act", ACT(xn[:, ft, c0:c0 + L], yy[:, 0:L], AF.Gelu_apprx_tanh),
                                 reads=[("yv", gi % 2)], writes=[("xn", ft, ci)])

                    NI = len(iters)
                    for s_ in range(NI + 2):
                        if s_ < NI:
                            st1(iters[s_])
                            st2(iters[s_])
                        if 0 <= s_ - 1 < NI:
                            st3(iters[s_ - 1])
                            st4(iters[s_ - 1])
                        if 0 <= s_ - 2 < NI:
                            st5(iters[s_ - 2])
                def store_state(src_ap, dst_ap, rkey):
                    i_ = rr[1] % 4
                    rr[1] += 1
                    bk_ = 6 + i_ % 2
                    p.op("pe", T(PS[bk_][0:32, 0:128], src_ap, ident[:]), reads=[rkey, "ident"], writes=[("ps", bk_)])
                    p.op("dve", CP(soutr[i_][:], PS[bk_][0:32, 0:128]), reads=[("ps", bk_)], writes=[("soutr", i_)])
                    p.dma("sp", dst_ap.rearrange("(b g) p -> b (g p)", g=2), soutr[i_][:], reads=[("soutr", i_)], key="soutr%d" % i_)
                p.barrier()
                store_state(carryP[:, :, 0], o_ssm_re_p[l], "carryAll")
                store_state(carryP[:, :, 1], o_ssm_im_p[l], "carryAll")
                for b in range(NS):
                    store_state(NSt[:, :, 0, b], o_ssm_re_s[l, b], "carryAll")
                    store_state(NSt[:, :, 1, b], o_ssm_im_s[l, b], "carryAll")
                if l + 1 < NA:
                    p.op("dve", lambda e: e.memset(carryP[:], 0.0), writes=["carryAll"])
                p.barrier()
            with ExitStack() as ph:
                alloc_w(ph)
                glbp = sbt(ph, "glbp", [128, 16], F32)
                gt = [sbt(ph, "gt%d" % i, [128, 512], F32) for i in range(2)]
                W8, sA = B["W8"], B["sA"]
                glnat = sbt(ph, "glnat", [16, 128], F32)
                p.dma("sp", glnat[:], glu_b[l].rearrange("(f p) -> f p", p=128), writes=["gnat2"], key="glb")
                p.op("pe", T(PS[7][:, 0:16], glnat[:], ident[0:16, 0:16]), reads=["gnat2", "ident"], writes=[("ps", 7)])
                p.op("dve", CP(glbp[:], PS[7][:, 0:16]), reads=[("ps", 7)], writes=["glbp"])
                for c4 in range(2):
                    s1 = load_w8(glu_w[l][:, c4 * 512:(c4 + 1) * 512])
                    s2 = load_w8(glu_w[l][:, D + c4 * 512:D + (c4 + 1) * 512])
                    for j in range(4):
                        fo = c4 * 4 + j
                        for ci, (c0, nn) in enumerate(CBS):
                            u = unit_ctr[0]
                            unit_ctr[0] += 1
                            ba, bb = u % 2, 2 + u % 2
                            for kt in range(8):
                                p.op("pe", MM(PS[ba][:, 0:nn], W8[s1][:, kt, j * 128:(j + 1) * 128], xn[:, kt, c0:c0 + nn], kt == 0, kt == 7),
                                     reads=[("w8", s1), ("xn", kt, ci)], writes=[("ps", ba)])
                            for kt in range(8):
                                p.op("pe", MM(PS[bb][:, 0:nn], W8[s2][:, kt, j * 128:(j + 1) * 128], xn[:, kt, c0:c0 + nn], kt == 0, kt == 7),
                                     reads=[("w8", s2), ("xn", kt, ci)], writes=[("ps", bb)])
                            p.op("act", ACT(sA[u % 2][:, 0:nn], PS[bb][:, 0:nn], AF.Sigmoid, bias=glbp[:, 8 + fo:9 + fo]),
                                 reads=[("ps", bb), "glbp"], writes=[("sA", u % 2)])
                            p.op("dve", STT(gt[u % 2][:, 0:nn], PS[ba][:, 0:nn], glbp[:, fo:fo + 1], sA[u % 2][:, 0:nn], ALU.add, ALU.mult),
                                 reads=[("ps", ba), ("sA", u % 2), "glbp"], writes=[("gt", u % 2)])
                            p.op("dve", TT(resid[:, fo, c0:c0 + nn], resid[:, fo, c0:c0 + nn], gt[u % 2][:, 0:nn], ALU.add),
                                 reads=[("gt", u % 2), ("resid", fo, ci)], writes=[("resid", fo, ci)])
                p.barrier()

        DILS = (1, 4, 16)
        WINS = (128, 512, 2048)

        def kv_phase():
            with ExitStack() as ph:
                alloc_norm(ph)
                alloc_w(ph)
                W8 = B["W8"]
                stg = [sbt(ph, "kvstg%d" % i, [128, 512], F32) for i in range(2)]
                ktst = [sbt(ph, "ktst%d" % i, [128, NT], BF16) for i in range(2)]
                rmsnorm(12)
                tiles = [(tt * 128, 128, tt // 4) for tt in range(16)] + [(SEQ, NS, 4)]
                it = 0
                for c6 in range(6):
                    g, kv = c6 // 2, c6 % 2
                    sl = load_w8(attn_w_kv[:, c6 * 512:(c6 + 1) * 512])
                    for (c0, M, ci) in tiles:
                        if kv == 0 and c0 < SEQ - WINS[g]:
                            continue
                        bk = it % 2
                        it += 1
                        for kt in range(8):
                            p.op("pe", MM(PS[bk][0:M, :], xn[:, kt, c0:c0 + M], W8[sl][:, kt, :], kt == 0, kt == 7),
                                 reads=[("w8", sl), ("xn", kt, ci)], writes=[("ps", bk)])
                        p.op("act", ACT(stg[bk][0:M, :], PS[bk][0:M, :], AF.Copy), reads=[("ps", bk)], writes=[("stg", bk)])
                        p.dma("sp", kvd[g, kv, c0:c0 + M, :], stg[bk][0:M, :], reads=[("stg", bk)], writes=[("kvd", g, kv, c0)], key="stg%d" % bk)
                for g in range(3):
                    sl = load_w8(attn_w_kv[:, g * 1024:g * 1024 + 512])
                    for j in range(4):
                        st_ = ktst[j % 2]
                        for ci, (c0, nn) in enumerate(CBS):
                            bk = 2 + it % 2
                            it += 1
                            for kt in range(8):
                                p.op("pe", MM(PS[bk][:, 0:nn], W8[sl][:, kt, j * 128:(j + 1) * 128], xn[:, kt, c0:c0 + nn], kt == 0, kt == 7),
                                     reads=[("w8", sl), ("xn", kt, ci)], writes=[("ps", bk)])
                            p.op("act", ACT(st_[:, c0:c0 + nn], PS[bk][:, 0:nn], AF.Copy), reads=[("ps", bk)], writes=[("ktst", j % 2)])
                        p.dma("sp", ktd[g, j], st_[:], reads=[("ktst", j % 2)], writes=[("ktd", g, j)], key="ktst%d" % (j % 2))
                p.barrier()
                go = p.new_group()
                outs = ((0, o_kv128_p, o_kv128_s), (1, o_kv512_p, o_kv512_s), (2, o_kv2048_p, o_kv2048_s))
                for (g, op_, os_) in outs:
                    w_ = WINS[g]
                    for kv in range(2):
                        p.dma("sp", op_[:, kv, :], kvd[g, kv, SEQ - w_:SEQ, :], key="kvout", group=go)
                        p.dma("sp", os_[:, kv, :], kvd[g, kv, SEQ:NT, :], key="kvout", group=go)
                p.barrier()

        def attn_layer(b_):
            layer = NA + b_
            with ExitStack() as ph:
                alloc_norm(ph)
                rmsnorm(4 + layer)
                p.barrier()
            with ExitStack() as ph:
                B["W8"] = [sbt(ph, "aw8_%d" % i, [128, 8, 256], BF16) for i in range(2)]
                B["WO"] = [sbt(ph, "awo_%d" % i, [128, 4, 1024], BF16) for i in range(1)]
                B["hbuf"] = sbt(ph, "ahbuf", [128, 4, NT], BF16)
                W8, WO = B["W8"], B["WO"]
                sm = lambda name, shape, dt=F32: sbt(ph, "at_" + name, shape, dt)
                qT = sm("qT", [128, 2, 2, NT], BF16)
                kT = sm("kT", [128, 2, NT], BF16)
                ACC4 = sm("ACC4", [128, 4, NT])
                mrg = B["hbuf"]
                bm = sm("bm", [128, 2, 512]); bmc = sm("bmc", [128, 4])
                Vt = [sm("V%d" % i, [128, 2, 4, 64], BF16) for i in range(4)]
                for i_ in range(4):
                    p.op("dve", (lambda t_: lambda e: e.memset(t_[:, :, 1:3, :], 1.0))(Vt[i_]), writes=[("V", i_)])
                sbuf_s = [sm("ss%d" % i, [128, 512]) for i in range(4)]
                pT = [sm("pT%d" % i, [128, 512], BF16) for i in range(4)]
                p.op("dve", lambda e: e.memset(qT[:], 0.0), writes=["qT"])
                kcf = sm("kcf", [128, 256]); kcT = sm("kcT", [128, 2, 128], BF16)
                ones64 = ones_bf[:, 0:64]
                it = [0]
                vctr = [0]
                qslots = {}

                SB = (0, 1, 6, 7)
                pend = [None]

                def unit_back(st):
                    bufs, nq, acc_cols, par = st
                    nkb = len(bufs)
                    ab_ = 2 + par
                    AL = PS[ab_]
                    for h4 in range(4):
                        hp2, hh = h4 // 2, h4 % 2
                        for kbi, (u, vt, vkey, nk) in enumerate(bufs):
                            p.op("pe", MM(AL[:, h4 * nq:(h4 + 1) * nq], vt[0:nk, hp2, 2 * hh:2 * hh + 2, :],
                                          pT[u][0:nk, h4 * nq:(h4 + 1) * nq], kbi == 0, kbi == nkb - 1),
                                 reads=[("pT", u), vkey], writes=[("ps", ab_)])
                    d_ = acc_cols(ACC4)
                    p.op("dve", TT(d_, d_, AL[:, 0:4 * nq].rearrange("p (a b) -> p a b", a=4), ALU.add),
                         reads=[("ps", ab_), "acc"], writes=["acc"])

                def flush():
                    if pend[0] is not None:
                        unit_back(pend[0])
                        pend[0] = None

                def unit(g, ps_, keysets, qap, nq, acc_cols):
                    par = it[0] % 2
                    it[0] += 1
                    bufs = []
                    for kbi, (kfn, vt, vkey, nk, bias) in enumerate(keysets):
                        u = par * 2 + kbi
                        bank = PS[SB[u]]
                        for h4 in range(4):
                            hp2, hh = h4 // 2, h4 % 2
                            p.op("pe", MM(bank[0:nk, h4 * nq:(h4 + 1) * nq], kfn(hh, hp2), qap(hh, hp2), True, True),
                                 reads=["qT", "kT", "kcT"], writes=[("ps", SB[u])])
                        if bias is not None:
                            p.op("dve", STT(sbuf_s[u][0:nk, 0:4 * nq], bank[0:nk, 0:4 * nq], 0.125, bias, ALU.mult, ALU.add),
                                 reads=[("ps", SB[u]), "bm"], writes=[("ss", u)])
                        else:
                            p.op("dve", TS(sbuf_s[u][0:nk, 0:4 * nq], bank[0:nk, 0:4 * nq], 0.125, None, ALU.mult),
                                 reads=[("ps", SB[u])], writes=[("ss", u)])
                        p.op("act", ACT(pT[u][0:nk, 0:4 * nq], sbuf_s[u][0:nk, 0:4 * nq], AF.Exp), reads=[("ss", u)], writes=[("pT", u)])
                        bufs.append((u, vt, vkey, nk))
                    st = (bufs, nq, acc_cols, par)
                    flush()
                    pend[0] = st

                for ps_ in range(2):
                    p.op("dve", lambda e: e.memset(ACC4[:], 0.0), writes=["acc"])
                    for g in range(3):
                        d, w_ = DILS[g], WINS[g]
                        oi = ps_ * 3 + g
                        if oi == 0:
                            qslots[0] = load_w8(attn_w_q[b_][:, 0:256])
                        sl = qslots[oi]
                        for j in range(2):
                            for ci, (c0, nn) in enumerate(CBS):
                                bk = 4 + it[0] % 2
                                it[0] += 1
                                for kt in range(8):
                                    p.op("pe", MM(PS[bk][:, 0:nn], W8[sl][:, kt, j * 128:(j + 1) * 128], xn[:, kt, c0:c0 + nn], kt == 0, kt == 7),
                                         reads=[("w8", sl), ("xn", kt, ci)], writes=[("ps", bk)])
                                p.op("act", ACT(qT[0:64, 0, j, c0:c0 + nn], PS[bk][0:64, 0:nn], AF.Copy), reads=[("ps", bk)], writes=["qT"])
                                p.op("act", ACT(qT[64:128, 1, j, c0:c0 + nn], PS[bk][64:128, 0:nn], AF.Copy), reads=[("ps", bk)], writes=["qT"])
                        if oi + 1 < 6:
                            ps_n, g_n = (oi + 1) // 3, (oi + 1) % 3
                            qslots[oi + 1] = load_w8(attn_w_q[b_][:, g_n * 512 + ps_n * 256:g_n * 512 + ps_n * 256 + 256])
                        gk = p.new_group()
                        for j in range(2):
                            p.dma("sp", kT[:, j, :], ktd[g, ps_ * 2 + j], writes=["kT"], key="kTld", group=gk)
                        gbm = p.new_group()
                        p.dma("sp", bm[:], c_bias[g, ps_].rearrange("k p c -> p k c"), writes=["bm"], key="bmld", group=gbm)
                        p.dma("sp", bmc[:], c_biasc[g, ps_], writes=["bm"], key="bmld", group=gbm)
                        nres = SEQ // d
                        nb = nres // 128
                        for r in range(d):
                            prev = None
                            for n in range(nb):
                                start = r + d * 128 * n
                                cols = slice(start, start + 127 * d + 1, d)
                                vi = vctr[0] % 4
                                vctr[0] += 1
                                gv = p.new_group()
                                vsrc = kvd[g, 1, start:start + 127 * d + 1:d, ps_ * 256:(ps_ + 1) * 256].rearrange("k (a b e) -> k a b e", a=2, b=2)
                                for hh_ in range(2):
                                    p.dma("pool", Vt[vi][:, :, 3 * hh_, :], vsrc[:, :, hh_, :], writes=[("V", vi)], key="V%d" % vi, group=gv)
                                kfn = (lambda cs: lambda hh, hp2: kT[:, hp2, cs])(cols)
                                qfn = (lambda cs: lambda hh, hp2: qT[:, hh, hp2, cs])(cols)
                                cur = (kfn, Vt[vi], ("V", vi), 128, bm[:, 1, :])
                                ks = [cur] if prev is None else [(prev[0], prev[1], prev[2], 128, bm[:, 0, :]), cur]
                                unit(g, ps_, ks, qfn, 128, (lambda cs: lambda dst: dst[:, :, cs])(cols))
                                prev = cur
                        for b in range(NS):
                            vi = vctr[0] % 4
                            vctr[0] += 1
                            vj = vctr[0] % 4
                            vctr[0] += 1
                            cch = caches[g]
                            p.dma("sp", kcf[:], cch[b, 0:w_ - d + 1:d, 0, ps_ * 256:(ps_ + 1) * 256], writes=["kcf"], key="kcf")
                            gv1, gv2 = p.new_group(), p.new_group()
                            vs1 = cch[b, 0:w_ - d + 1:d, 1, ps_ * 256:(ps_ + 1) * 256].rearrange("k (a b e) -> k a b e", a=2, b=2)
                            vs2 = kvd[g, 1, SEQ + b:SEQ + b + 1, ps_ * 256:(ps_ + 1) * 256].rearrange("k (a b e) -> k a b e", a=2, b=2)
                            for hh_ in range(2):
                                p.dma("pool", Vt[vi][:, :, 3 * hh_, :], vs1[:, :, hh_, :], writes=[("V", vi)], key="V%d" % vi, group=gv1)
                                p.dma("pool", Vt[vj][0:1, :, 3 * hh_, :], vs2[:, :, hh_, :], writes=[("V", vj)], key="V%d" % vj, group=gv2)
                            for j in range(2):
                                p.op("pe", T(PS[4][:, j * 128:(j + 1) * 128], kcf[:, j * 128:(j + 1) * 128], ident[:]),
                                     reads=["kcf", "ident"], writes=[("ps", 4)])
                            p.op("dve", CP(kcT[:], PS[4][:, 0:256].rearrange("p (a b) -> p a b", a=2)), reads=[("ps", 4)], writes=["kcT"])
                            col = SEQ + b
                            ks = [((lambda hh, hp2: kcT[:, hp2, :]), Vt[vi], ("V", vi), 128, bmc[:]),
                                  ((lambda c_: lambda hh, hp2: kT[:, hp2, c_:c_ + 1])(col), Vt[vj], ("V", vj), 1, None)]
                            qfn = (lambda c_: lambda hh, hp2: qT[:, hh, hp2, c_:c_ + 1])(col)
                            unit(g, ps_, ks, qfn, 1, (lambda c_: lambda dst: dst[:, :, c_:c_ + 1])(col))
                        flush()
                    flush()
                    mi = 0
                    for hp2 in range(2):
                        for ci, (c0, nn) in enumerate(CBS):
                            bk = 4 + mi % 2
                            tl, tr = sbuf_s[mi % 2], sbuf_s[2 + mi % 2]
                            mi += 1
                            p.op("pe", MM(PS[bk][0:64, 0:nn], ident[:, 64:128], ACC4[:, 2 * hp2, c0:c0 + nn], True, True),
                                 reads=["acc", "ident"], writes=[("ps", bk)])
                            p.op("pe", MM(PS[bk][64:128, 0:nn], ident[:, 0:64], ACC4[:, 2 * hp2 + 1, c0:c0 + nn], True, True),
                                 reads=["acc", "ident"], writes=[("ps", bk)])
                            p.op("act", ACT(tl[:, 0:nn], PS[bk][:, 0:nn], AF.Ln), reads=[("ps", bk)], writes=[("ss", mi % 2)])
                            p.op("act", ACT(tr[:, 0:nn], tl[:, 0:nn], AF.Exp, scale=-1.0), reads=[("ss", mi % 2)], writes=[("ss", 2 + mi % 2)])
                            hp = ps_ * 2 + hp2
                            p.op("dve", TT(mrg[0:64, hp, c0:c0 + nn], ACC4[0:64, 2 * hp2, c0:c0 + nn], tr[0:64, 0:nn], ALU.mult),
                                 reads=["acc", ("ss", 2 + mi % 2)], writes=[("h", hp, ci)])
                            p.op("dve", TT(mrg[64:128, hp, c0:c0 + nn], ACC4[64:128, 2 * hp2 + 1, c0:c0 + nn], tr[64:128, 0:nn], ALU.mult),
                                 reads=["acc", ("ss", 2 + mi % 2)], writes=[("h", hp, ci)])
                so_ = load_wo(attn_w_o[b_])
                for fo in range(8):
                    for ci, (c0, nn) in enumerate(CBS):
                        dsel = 4 + dn_ctr[0] % 2
                        dn_ctr[0] += 1
                        for j in range(4):
                            p.op("pe", MM(PS[dsel][:, 0:nn], WO[so_][:, j, fo * 128:(fo + 1) * 128], mrg[:, j, c0:c0 + nn], j == 0, j == 3),
                                 reads=[("wo", so_), ("h", j, ci)], writes=[("ps", dsel)])
                        p.op("dve", TT(resid[:, fo, c0:c0 + nn], resid[:, fo, c0:c0 + nn], PS[dsel][:, 0:nn], ALU.add),
                             reads=[("ps", dsel), ("resid", fo, ci)], writes=[("resid", fo, ci)])
                p.barrier()

        if stage == "A":
            ffn_phase(0, 0, 0)
        elif stage == "B":
            s5_layer(0)
        elif stage == "C":
            kv_phase()
            attn_layer(0)
        elif stage == "full":
            for l in range(NA):
                ffn_phase(l, 0, l)
                s5_layer(l)
                ffn_phase(l, 1, 8 + l)
            kv_phase()
            for b_ in range(2):
                ffn_phase(NA + b_, 0, NA + b_)
                attn_layer(b_)
                ffn_phase(NA + b_, 1, 8 + NA + b_)

        if debug:
            p.dma("sp", dbg, resid[:], reads=ALLRES, key="dbg")

        with ExitStack() as ph:
            alloc_norm(ph)
            xtok = [sbt(ph, "xtok%d" % i, [128, 1024], F32) for i in range(4)]
            rmsnorm(13, out_f32=True)
            for tt in range(16):
                s = tt % 4
                for half in range(2):
                    bank = PS[6 + half]
                    for f4 in range(4):
                        ft = half * 4 + f4
                        p.op("pe", T(bank[:, f4 * 128:(f4 + 1) * 128], resid[:, ft, tt * 128:(tt + 1) * 128], ident[:]),
                             reads=[("resid", ft, tt // 4), "ident"], writes=[("ps", 6 + half)])
                    p.op("act" if half == 0 else "dve",
                         ACT(xtok[s][:, half * 512:(half + 1) * 512], bank[:, :], AF.Copy) if half == 0
                         else CP(xtok[s][:, half * 512:(half + 1) * 512], bank[:, :]),
                         reads=[("ps", 6 + half)], writes=[("xtok", s, half)])
                p.dma("sp", y_prompt[tt * 128:(tt + 1) * 128, :], xtok[s][:], reads=[("xtok", s, 0), ("xtok", s, 1)], key="xtok%d" % s)
            for ft in range(8):
                p.op("pe", T(PS[6][0:NS, ft * 128:(ft + 1) * 128] if ft < 4 else PS[7][0:NS, (ft - 4) * 128:(ft - 3) * 128],
                             resid[:, ft, SEQ:NT], ident[:]),
                     reads=[("resid", ft, 4), "ident"], writes=[("ps", 6 if ft < 4 else 7)])
            for half in range(2):
                p.op("act", ACT(xtok[0][0:NS, half * 512:(half + 1) * 512], PS[6 + half][0:NS, :], AF.Copy),
                     reads=[("ps", 6 + half)], writes=[("xtok", 0, half)])
            p.dma("sp", y_sample, xtok[0][0:NS, :], reads=[("xtok", 0, 0), ("xtok", 0, 1)], key="xtok0")

        p.emit()
    return nc


def _alibi():
    n = 24
    s = np.exp2(-8.0 * np.arange(1, n + 1, dtype=np.float64) / n)
    return s.reshape(3, 8)


def host_consts():
    f = np.float32
    tri = (np.arange(128)[:, None] <= np.arange(128)[None, :]).astype(f)
    sl = _alibi()
    dils = (1, 4, 16)
    k = np.arange(128)[:, None].astype(np.float64)
    q = np.arange(128)[None, :].astype(np.float64)
    cb = np.zeros((3, 2, 2, 128, 4, 128), np.float64)
    cc = np.zeros((3, 2, 128, 4), np.float64)
    for g in range(3):
        for ps in range(2):
            for h4 in range(4):
                s_ = sl[g, ps * 4 + h4] * dils[g]
                dprev = 128 + q - k
                cb[g, ps, 0, :, h4, :] = np.where(dprev <= 128, -s_ * dprev, -30000.0)
                dcur = q - k
                cb[g, ps, 1, :, h4, :] = np.where(dcur >= 0, -s_ * dcur, -30000.0)
                cc[g, ps, :, h4] = -s_ * (128 - np.arange(128))
    return {
        "c_ident": np.eye(128, dtype=f),
        "c_tri": np.concatenate([tri, -tri], axis=1).astype(f),
        "c_maskg": (np.arange(128)[:, None] // 16 == np.arange(8)[None, :]).astype(f),
        "c_bias": cb.reshape(3, 2, 2, 128, 512).astype(f),
        "c_biasc": cc.astype(f),
    }


_NC_CACHE = {}


def kernel(**inputs):
    if "nc" not in _NC_CACHE:
        _NC_CACHE["nc"] = build("full", debug=False)
    nc = _NC_CACHE["nc"]
    consts = host_consts()
    A = {k: np.ascontiguousarray(np.asarray(v)) for k, v in inputs.items()}
    in_maps = []
    for c in range(8):
        m = dict(consts)
        for k, v in A.items():
            if k == "x_prompt":
                m[k] = np.ascontiguousarray(v[c])
            elif k == "x_sample":
                m[k] = np.ascontiguousarray(v[4 * c:4 * c + 4, 0])
            elif k in ("state_ssm_re", "state_ssm_im"):
                m[k] = np.ascontiguousarray(v[:, 4 * c:4 * c + 4])
            elif k.startswith("cache_kv"):
                m[k] = np.ascontiguousarray(v[4 * c:4 * c + 4].reshape(4, v.shape[1], 2, 512))
            else:
                m[k] = v
        in_maps.append(m)
    res = run_bass_kernel_spmd(nc, in_maps, core_ids=list(range(8)))
    R = res.results
    f = np.float32
    y_prompt = np.stack([R[c]["y_prompt"] for c in range(8)]).astype(f)
    y_sample = np.concatenate([R[c]["y_sample"] for c in range(8)])[:, None, :].astype(f)
    srp = np.stack([R[c]["ssm_re_prompt"] for c in range(8)], axis=1).astype(f)
    sip = np.stack([R[c]["ssm_im_prompt"] for c in range(8)], axis=1).astype(f)
    srs = np.concatenate([R[c]["ssm_re_sample"] for c in range(8)], axis=1).astype(f)
    sis = np.concatenate([R[c]["ssm_im_sample"] for c in range(8)], axis=1).astype(f)
    kvp = [np.stack([R[c][n].reshape(-1, 2, 8, 64) for c in range(8)]).astype(f)
           for n in ("kv_w128_prompt", "kv_w512_prompt", "kv_w2048_prompt")]
    kvs = [np.concatenate([R[c][n].reshape(4, 1, 2, 8, 64) for c in range(8)]).astype(f)
           for n in ("kv_w128_sample", "kv_w512_sample", "kv_w2048_sample")]
    return (y_prompt, y_sample, srp, sip, kvp[0], kvp[1], kvp[2], srs, sis, kvs[0], kvs[1], kvs[2])
```

```python
from contextlib import ExitStack
import math
import os
import numpy as np
import ml_dtypes
import concourse.bass as bass
import concourse.mybir as mybir
from concourse.bass_utils import run_bass_kernel_spmd

F32 = mybir.dt.float32
BF16 = mybir.dt.bfloat16
I32 = mybir.dt.int32
AF = mybir.ActivationFunctionType
ALU = mybir.AluOpType

SEM_LIMIT = 30000

D = 1024
SEQ = 2048
NS = 4
NT = SEQ + NS
DFF = 2816
NFF = DFF // 128
DEPTH = 4
NA = 2
EPS = 1e-6
CBS = [(0, 512), (512, 512), (1024, 512), (1536, 512), (2048, NS)]


class Op:
    __slots__ = ("eng", "fn", "deps", "dma_key", "dma_group", "idx", "signal", "ticket")

    def __init__(self, eng, fn, deps, dma_key, dma_group, idx):
        self.eng = eng
        self.fn = fn
        self.deps = deps
        self.dma_key = dma_key
        self.dma_group = dma_group
        self.idx = idx
        self.signal = False
        self.ticket = None


class Prog:
    ENGS = ("pe", "act", "dve", "pool", "sp")

    def __init__(self, nc):
        self.nc = nc
        self.ops = []
        self.last_w = {}
        self.readers = {}
        self.grp_ctr = 0
        self.last_eng = {}
        self.dma_since = []
        self.bar_deps = set()
        self.bar_pending = set()

    def barrier(self):
        self.bar_deps = set(self.last_eng.values()) | set(self.dma_since)
        self.bar_pending = set(self.ENGS)
        self.dma_since = []

    def op(self, eng, fn, reads=(), writes=(), dma_key=None, dma_group=None):
        deps = set()
        if eng in self.bar_pending:
            deps |= self.bar_deps
            self.bar_pending.discard(eng)
        for r in reads:
            w = self.last_w.get(r)
            if w is not None:
                deps.add(w)
        for w_ in writes:
            w = self.last_w.get(w_)
            if w is not None:
                deps.add(w)
            for rd in self.readers.get(w_, ()):
                deps.add(rd)
        idx = len(self.ops)
        if dma_key is not None:
            deps = {d for d in deps if not (self.ops[d].dma_key == dma_key and self.ops[d].dma_group == dma_group)}
        o = Op(eng, fn, deps, dma_key, dma_group, idx)
        self.ops.append(o)
        if dma_key is None:
            self.last_eng[eng] = idx
        else:
            self.dma_since.append(idx)
        for r in reads:
            self.readers.setdefault(r, []).append(idx)
        for w_ in writes:
            self.last_w[w_] = idx
            self.readers[w_] = []
        return idx

    def new_group(self):
        self.grp_ctr += 1
        return self.grp_ctr

    def dma(self, eng, out, in_, reads=(), writes=(), key=None, group=None, **kw):
        assert key is not None
        if group is None:
            group = self.new_group()
        return self.op(eng, lambda e: e.dma_start(out=out, in_=in_, **kw), reads, writes,
                       dma_key=key, dma_group=group)

    def emit(self):
        nc = self.nc
        ops = self.ops

        same_all = False

        def pe_pe(do, o):
            if do.dma_key is not None or o.dma_key is not None or do.eng != o.eng:
                return False
            return do.eng == "pe" or same_all

        for o in ops:
            for d in o.deps:
                if not pe_pe(ops[d], o):
                    ops[d].signal = True
        with ExitStack() as es:
            sems = {}

            def get_sem(name):
                if name not in sems:
                    sems[name] = es.enter_context(nc.semaphore(name))
                return sems[name]

            eng_cnt = {e: 0 for e in self.ENGS}
            key_cum, key_epoch, grp_size, grp_end = {}, {}, {}, {}
            for o in ops:
                if o.dma_key is not None:
                    g = (o.dma_key, o.dma_group)
                    grp_size[g] = grp_size.get(g, 0) + 1
            for o in ops:
                if o.dma_key is not None:
                    k = o.dma_key
                    g = (k, o.dma_group)
                    if g not in grp_end:
                        cum = key_cum.get(k, 0)
                        ep = key_epoch.get(k, 0)
                        if cum + 16 * grp_size[g] > SEM_LIMIT:
                            ep += 1
                            cum = 0
                        cum += 16 * grp_size[g]
                        key_cum[k] = cum
                        key_epoch[k] = ep
                        grp_end[g] = ("d_%s_%d" % (k, ep), cum)
                    o.ticket = grp_end[g]
                elif o.signal:
                    c = eng_cnt[o.eng]
                    eng_cnt[o.eng] = c + 1
                    o.ticket = ("e_%s_%d" % (o.eng, c // SEM_LIMIT), c % SEM_LIMIT + 1)
            last_tickets = {}
            for o in ops:
                if o.ticket is not None:
                    get_sem(o.ticket[0])
                    last_tickets[o.ticket[0]] = max(last_tickets.get(o.ticket[0], 0), o.ticket[1])
            per_eng = {e: [o for o in ops if o.eng == e] for e in self.ENGS}

            with nc.Block() as block:
                def run_engine(e, eh, final=False):
                    waited = {}
                    for o in per_eng[e]:
                        need = {}
                        for d in o.deps:
                            do = ops[d]
                            if do.ticket is None or pe_pe(do, o):
                                continue
                            s, v = do.ticket
                            if need.get(s, 0) < v:
                                need[s] = v
                        for s, v in need.items():
                            if waited.get(s, 0) >= v:
                                continue
                            eh.wait_ge(sems[s], v)
                            waited[s] = v
                        ins = o.fn(eh)
                        if o.dma_key is not None:
                            ins.then_inc(sems[o.ticket[0]], 16)
                        elif o.signal:
                            ins.then_inc(sems[o.ticket[0]], 1)
                    if final:
                        for s, v in last_tickets.items():
                            if waited.get(s, 0) < v:
                                eh.wait_ge(sems[s], v)

                @block.tensor
                def _(eh):
                    run_engine("pe", eh)

                @block.scalar
                def _(eh):
                    run_engine("act", eh)

                @block.vector
                def _(eh):
                    run_engine("dve", eh)

                @block.gpsimd
                def _(eh):
                    run_engine("pool", eh)

                @block.sync
                def _(eh):
                    run_engine("sp", eh, final=True)


def build(stage="full", debug=False):
    nc = bass.Bass("TRN2", target_bir_lowering=False)

    def din(name, shape, dt=F32):
        return nc.dram_tensor(name, list(shape), dt, kind="ExternalInput").ap()

    def dout(name, shape, dt=F32):
        return nc.dram_tensor(name, list(shape), dt, kind="ExternalOutput").ap()

    x_prompt = din("x_prompt", [SEQ, D])
    x_sample = din("x_sample", [NS, D])
    norm_ffn1 = din("norm_ffn1", [DEPTH, D])
    norm_mix = din("norm_mix", [DEPTH, D])
    norm_ffn2 = din("norm_ffn2", [DEPTH, D])
    norm_kv = din("norm_kv", [D])
    norm_final = din("norm_final", [D])
    ffn_w_in = [din("ffn1_w_in", [DEPTH, D, 2 * DFF]), din("ffn2_w_in", [DEPTH, D, 2 * DFF])]
    ffn_w_out = [din("ffn1_w_out", [DEPTH, DFF, D]), din("ffn2_w_out", [DEPTH, DFF, D])]
    state_re = din("state_ssm_re", [NA, NS, 64, 64])
    state_im = din("state_ssm_im", [NA, NS, 64, 64])
    ssm_log_dt = din("ssm_log_dt", [NA, 64])
    ssm_lre = din("ssm_lambda_re", [NA, 64, 64])
    ssm_lim = din("ssm_lambda_im", [NA, 64, 64])
    ssm_bre = din("ssm_b_re", [NA, 64, 64, 16])
    ssm_bim = din("ssm_b_im", [NA, 64, 64, 16])
    ssm_cre = din("ssm_c_re", [NA, 64, 16, 64])
    ssm_cim = din("ssm_c_im", [NA, 64, 16, 64])
    ssm_d = din("ssm_d", [NA, 64, 16])
    glu_w = din("glu_w", [NA, D, 2 * D])
    glu_b = din("glu_b", [NA, 2 * D])
    attn_w_q = din("attn_w_q", [2, D, 1536])
    attn_w_kv = din("attn_w_kv", [D, 3072])
    attn_w_o = din("attn_w_o", [2, 512, D])
    caches = [din("cache_kv_w128", [NS, 128, 2, 512]), din("cache_kv_w512", [NS, 512, 2, 512]), din("cache_kv_w2048", [NS, 2048, 2, 512])]
    c_bias = din("c_bias", [3, 2, 2, 128, 512])
    c_biasc = din("c_biasc", [3, 2, 128, 4])
    kvd = nc.dram_tensor("kvd", [3, 2, NT, 512], F32, kind="Internal").ap()
    ktd = nc.dram_tensor("ktd", [3, 4, 128, NT], BF16, kind="Internal").ap()
    o_kv128_p = dout("kv_w128_prompt", [128, 2, 512])
    o_kv512_p = dout("kv_w512_prompt", [512, 2, 512])
    o_kv2048_p = dout("kv_w2048_prompt", [2048, 2, 512])
    o_kv128_s = dout("kv_w128_sample", [NS, 2, 512])
    o_kv512_s = dout("kv_w512_sample", [NS, 2, 512])
    o_kv2048_s = dout("kv_w2048_sample", [NS, 2, 512])
    c_ident = din("c_ident", [128, 128])
    c_tri = din("c_tri", [128, 256])
    c_maskg = din("c_maskg", [128, 8])
    y_prompt = dout("y_prompt", [SEQ, D])
    y_sample = dout("y_sample", [NS, D])
    o_ssm_re_p = dout("ssm_re_prompt", [NA, 64, 64])
    o_ssm_im_p = dout("ssm_im_prompt", [NA, 64, 64])
    o_ssm_re_s = dout("ssm_re_sample", [NA, NS, 64, 64])
    o_ssm_im_s = dout("ssm_im_sample", [NA, NS, 64, 64])
    dbg = dout("dbg", [128, 8, NT]) if debug else None

    with ExitStack() as es:
        uniq = [0]

        def sbt(stack, name, shape, dt):
            uniq[0] += 1
            return stack.enter_context(nc.sbuf_tensor("%s_%d" % (name, uniq[0]), list(shape), dt))

        p = Prog(nc)
        es.enter_context(nc.allow_non_contiguous_dma(reason="small strided parameter loads"))

        resid = sbt(es, "resid", [128, 8, NT], F32)
        xn = sbt(es, "xn", [128, 8, NT], BF16)
        ident = sbt(es, "ident", [128, 128], F32)
        ones_bf = sbt(es, "ones_bf", [128, 128], BF16)
        gains = sbt(es, "gains", [128, 8, 16], F32)
        carryP = sbt(es, "carryP", [128, 32, 2], F32)
        PS = [es.enter_context(nc.psum_tensor("ps%d" % i, [128, 512], F32)) for i in range(8)]
        B = {}

        def T(o_, i_, idn):
            return lambda e: e.transpose(out=o_, in_=i_, identity=idn)

        def MM(o_, l_, r_, st, sp_):
            return lambda e: e.matmul(o_, lhsT=l_, rhs=r_, start=st, stop=sp_)

        def CP(o_, i_):
            return lambda e: e.tensor_copy(out=o_, in_=i_)

        def ACT(o_, i_, f, **kw):
            return lambda e: e.activation(out=o_, in_=i_, func=f, **kw)

        def TT(o_, a_, b_, op):
            return lambda e: e.tensor_tensor(out=o_, in0=a_, in1=b_, op=op)

        def TS(o_, a_, s1, s2, op0, op1=None):
            if op1 is None:
                return lambda e: e.tensor_scalar(out=o_, in0=a_, scalar1=s1, scalar2=None, op0=op0)
            return lambda e: e.tensor_scalar(out=o_, in0=a_, scalar1=s1, scalar2=s2, op0=op0, op1=op1)

        def STT(o_, a_, sc, b_, op0, op1):
            return lambda e: e.scalar_tensor_tensor(out=o_, in0=a_, scalar=sc, in1=b_, op0=op0, op1=op1)

        def alloc_norm(st):
            B["sq"] = [sbt(st, "sq%d" % i, [128, 8, 512], BF16) for i in range(2)]
            B["lnt"] = sbt(st, "lnt", [128, 512], F32)
            B["rstd"] = [sbt(st, "rstd%d" % i, [128, 512], F32) for i in range(2)]

        def alloc_w(st):
            B["W8"] = [sbt(st, "w8_%d" % i, [128, 8, 512], BF16) for i in range(4)]
            B["WO"] = [sbt(st, "wo_%d" % i, [128, 4, 1024], BF16) for i in range(2)]
            B["hbuf"] = sbt(st, "hbuf", [128, 4, NT], BF16)
            B["sA"] = [sbt(st, "sA%d" % i, [128, 512], F32) for i in range(2)]

        ALLRES = [("resid", f, c) for f in range(8) for c in range(5)]

        p.dma("sp", ident[:], c_ident, writes=["ident"], key="init")
        p.op("dve", lambda e: e.memset(ones_bf[:], 1.0), writes=["ones"])
        p.op("dve", lambda e: e.memset(carryP[:], 0.0), writes=["carryP"])
        gst = ExitStack()
        gnat = sbt(gst, "gnat", [16, 1024], F32)
        g0 = p.new_group()
        p.dma("sp", gnat[0:4, :], norm_ffn1, writes=["gnat"], key="init2", group=g0)
        p.dma("sp", gnat[4:8, :], norm_mix, writes=["gnat"], key="init2", group=g0)
        p.dma("sp", gnat[8:12, :], norm_ffn2, writes=["gnat"], key="init2", group=g0)
        p.dma("sp", gnat[12:13, :], norm_kv.rearrange("(a f) -> a f", a=1), writes=["gnat"], key="init2", group=g0)
        p.dma("sp", gnat[13:14, :], norm_final.rearrange("(a f) -> a f", a=1), writes=["gnat"], key="init2", group=g0)
        for ft in range(8):
            p.op("pe", T(PS[6][:, ft * 16:ft * 16 + 14], gnat[0:14, ft * 128:(ft + 1) * 128], ident[0:14, 0:14]),
                 reads=["gnat", "ident"], writes=[("ps", 6)])
        p.op("dve", CP(gains[:, :, 0:14], PS[6][:, 0:128].rearrange("p (a b) -> p a b", a=8)[:, :, 0:14]),
             reads=[("ps", 6)], writes=["gains"])
        p.barrier()
        gst.close()

        with ExitStack() as ph:
            xtok = [sbt(ph, "xtok%d" % i, [128, 1024], F32) for i in range(4)]
            for tt in range(16):
                s = tt % 4
                p.dma("sp", xtok[s][:], x_prompt[tt * 128:(tt + 1) * 128, :], writes=[("xtok", s)], key="xtok%d" % s)
                for half in range(2):
                    bank = PS[6 + half]
                    for f4 in range(4):
                        ft = half * 4 + f4
                        p.op("pe", T(bank[:, f4 * 128:(f4 + 1) * 128], xtok[s][:, ft * 128:(ft + 1) * 128], ident[:]),
                             reads=[("xtok", s), "ident"], writes=[("ps", 6 + half)])
                    dst_ = resid[:, half * 4:half * 4 + 4, tt * 128:(tt + 1) * 128]
                    src_ = bank[:, :].rearrange("p (a b) -> p a b", a=4)
                    p.op("dve" if half == 0 else "act", CP(dst_, src_) if half == 0 else ACT(dst_, src_, AF.Copy),
                         reads=[("ps", 6 + half)], writes=[("resid", half * 4 + f, tt // 4) for f in range(4)])
            p.dma("sp", xtok[0][0:NS, :], x_sample, writes=[("xtok", 0)], key="xtok0")
            for ft in range(8):
                p.op("pe", T(PS[6][:, ft * NS:(ft + 1) * NS], xtok[0][0:NS, ft * 128:(ft + 1) * 128], ident[0:NS, 0:NS]),
                     reads=[("xtok", 0), "ident"], writes=[("ps", 6)])
            p.op("dve", CP(resid[:, :, SEQ:NT], PS[6][:, 0:8 * NS].rearrange("p (a b) -> p a b", a=8)),
                 reads=[("ps", 6)], writes=[("resid", f, 4) for f in range(8)])
            p.barrier()

        norm_ctr = [0]

        def rmsnorm(gidx, out_f32=False):
            sq, lnt, rstd = B["sq"], B["lnt"], B["rstd"]
            for ci, (c0, n) in enumerate(CBS):
                k = norm_ctr[0]
                norm_ctr[0] += 1
                s = k % 2
                bank = PS[6 + s]
                p.op("act", ACT(sq[s][:, :, 0:n], resid[:, :, c0:c0 + n], AF.Square),
                     reads=[("resid", f, ci) for f in range(8)], writes=[("sq", s)])
                for kt in range(8):
                    p.op("pe", MM(bank[:, 0:n], ones_bf[:], sq[s][:, kt, 0:n], kt == 0, kt == 7),
                         reads=[("sq", s), "ones"], writes=[("ps", 6 + s)])
                p.op("act", ACT(lnt[:, 0:n], bank[:, 0:n], AF.Ln, scale=1.0 / D, bias=EPS),
                     reads=[("ps", 6 + s)], writes=["lnt"])
                p.op("act", ACT(rstd[s][:, 0:n], lnt[:, 0:n], AF.Exp, scale=-0.5), reads=["lnt"], writes=[("rstd", s)])
                for ft in range(8):
                    dst = resid[:, ft, c0:c0 + n] if out_f32 else xn[:, ft, c0:c0 + n]
                    wkey = ("resid", ft, ci) if out_f32 else ("xn", ft, ci)
                    p.op("dve", STT(dst, resid[:, ft, c0:c0 + n], gains[:, ft, gidx:gidx + 1], rstd[s][:, 0:n], ALU.mult, ALU.mult),
                         reads=[("resid", ft, ci), ("rstd", s), "gains"], writes=[wkey])

        w8_ctr = [0]
        wo_ctr = [0]
        unit_ctr = [0]
        dn_ctr = [0]

        def load_w8(src_ap):
            s = w8_ctr[0] % len(B["W8"])
            w8_ctr[0] += 1
            ncols = src_ap.shape[1]
            p.dma("pool", B["W8"][s][:, :, 0:ncols], src_ap.rearrange("(k p) c -> p k c", p=128),
                  writes=[("w8", s)], key="w8_%d" % s)
            return s

        def load_wo(src_ap):
            s = wo_ctr[0] % len(B["WO"])
            wo_ctr[0] += 1
            n = src_ap.shape[0] // 128
            p.dma("pool", B["WO"][s][:, 0:n, :], src_ap.rearrange("(k p) c -> p k c", p=128),
                  writes=[("wo", s)], key="wo_%d" % s)
            return s

        def ffn(layer, which, gidx):
            W8, WO, hbuf, sA = B["W8"], B["WO"], B["hbuf"], B["sA"]
            w_in = ffn_w_in[which][layer]
            w_out = ffn_w_out[which][layer]
            rmsnorm(gidx)
            parts = [(t0, min(4, NFF - t0)) for t0 in range(0, NFF, 4)]
            CBS_F = CBS
            for (t0, n) in parts:
                sa_ = load_w8(w_in[:, t0 * 128:(t0 + n) * 128])
                sb_ = load_w8(w_in[:, DFF + t0 * 128:DFF + (t0 + n) * 128])
                so_ = load_wo(w_out[t0 * 128:(t0 + n) * 128, :])
                for j in range(n):
                    for ci, (c0, nn) in enumerate(CBS_F):
                        u = unit_ctr[0]
                        unit_ctr[0] += 1
                        ba, bb = u % 2, 2 + u % 2
                        for kt in range(8):
                            p.op("pe", MM(PS[ba][:, 0:nn], W8[sa_][:, kt, j * 128:(j + 1) * 128], xn[:, kt, c0:c0 + nn], kt == 0, kt == 7),
                                 reads=[("w8", sa_), ("xn", kt, ci)], writes=[("ps", ba)])
                        for kt in range(8):
                            p.op("pe", MM(PS[bb][:, 0:nn], W8[sb_][:, kt, j * 128:(j + 1) * 128], xn[:, kt, c0:c0 + nn], kt == 0, kt == 7),
                                 reads=[("w8", sb_), ("xn", kt, ci)], writes=[("ps", bb)])
                        p.op("act", ACT(sA[u % 2][:, 0:nn], PS[ba][:, 0:nn], AF.Silu), reads=[("ps", ba)], writes=[("sA", u % 2)])
                        p.op("dve", TT(hbuf[:, j, c0:c0 + nn], PS[bb][:, 0:nn], sA[u % 2][:, 0:nn], ALU.mult),
                             reads=[("ps", bb), ("sA", u % 2)], writes=[("h", j, ci)])
                for fo in range(8):
                    for ci, (c0, nn) in enumerate(CBS_F):
                        dsel = 4 + dn_ctr[0] % 2
                        dn_ctr[0] += 1
                        for j in range(n):
                            p.op("pe", MM(PS[dsel][:, 0:nn], WO[so_][:, j, fo * 128:(fo + 1) * 128], hbuf[:, j, c0:c0 + nn], j == 0, j == n - 1),
                                 reads=[("wo", so_), ("h", j, ci)], writes=[("ps", dsel)])
                        p.op("dve", STT(resid[:, fo, c0:c0 + nn], PS[dsel][:, 0:nn], 0.5, resid[:, fo, c0:c0 + nn], ALU.mult, ALU.add),
                             reads=[("ps", dsel), ("resid", fo, ci)], writes=[("resid", fo, ci)])

        def ffn_phase(layer, which, gidx):
            with ExitStack() as ph:
                alloc_norm(ph)
                alloc_w(ph)
                ffn(layer, which, gidx)
                p.barrier()

        PI = math.pi
        seqs = [("P", c) for c in range(16)] + [("S", b) for b in range(NS)]

        def s5_layer(l):
            with ExitStack() as ph:
                alloc_norm(ph)
                rmsnorm(4 + l)
                p.barrier()
            with ExitStack() as ph:
                sm = lambda name, shape, dt=F32: sbt(ph, "s5_" + name, shape, dt)
                nat = sm("nat", [32, 128])
                tri_bf = sm("tri_bf", [128, 256], BF16)
                maskg = sm("maskg", [128, 8])
                p.dma("pool", tri_bf[:], c_tri, writes=["tri"], key="initp")
                p.dma("sp", maskg[:], c_maskg, writes=["maskg"], key="init3")
                ldt = sm("ldt", [32, 2])
                lr = sm("lr", [128, 32]); li = sm("li", [128, 32]); dt_ = sm("dt", [128, 32])
                t1 = sm("t1", [128, 32]); t2 = sm("t2", [128, 32]); t3 = sm("t3", [128, 32])
                mag = sm("mag", [128, 32]); imag = sm("imag", [128, 32])
                sn = sm("sn", [128, 32]); cs = sm("cs", [128, 32])
                ab = sm("ab", [128, 32, 2]); abi = sm("abi", [128, 32, 2])
                czr = sm("czr", [128, 32]); czi = sm("czi", [128, 32])
                dsk = sm("dsk", [128, 8])
                Ep = sm("Ep", [128, 16, 2, 128]); EmT = sm("EmT", [128, 16, 2, 128])
                Bw4 = sm("Bw4", [128, 16, 512], BF16)
                Cw = sm("Cw", [128, 16, 3, 128], BF16)
                XA = sm("XA", [128, 16, 128])
                d1 = sm("d1", [128, 16, 64]); d2 = sm("d2", [128, 16, 64])
                d3 = sm("d3", [128, 16, 64]); d4 = sm("d4", [128, 16, 64])
                bnr = sm("bnr", [128, 16, 16]); bni = sm("bni", [128, 16, 16])
                bbr = sm("bbr", [128, 16, 16]); bbi = sm("bbi", [128, 16, 16]); bt = sm("bt", [128, 16, 16])
                cnr = sm("cnr", [128, 4, 64]); cni = sm("cni", [128, 4, 64])
                qb = [sm("q%d" % i, [128, 512], BF16) for i in range(3)]
                wb = [sm("w%d" % i, [128, 4, 128], BF16) for i in range(3)]
                wS = [sm("wS%d" % i, [128, 2, NS], BF16) for i in range(3)]
                SS = sm("SS", [128, 32, 2, NS]); CPs = sm("CPs", [128, 32, 2, NS]); NSt = sm("NSt", [128, 32, 2, NS])
                sc1 = sm("sc1", [128, 32, NS]); sc2 = sm("sc2", [128, 32, NS])
                cab4 = sm("cab4", [128, 4, 2]); ct4 = sm("ct4", [128, 4, 2]); cu4 = sm("cu4", [128, 4, 2])
                yv = [sm("yv%d" % i, [128, 128]) for i in range(2)]
                cab = sm("cab", [128, 2]); ct = sm("ct", [128, 2]); cu = sm("cu", [128, 2])
                sout = sm("sout", [32, 128])

                natr = [sm("natr%d" % i, [32, 128]) for i in range(4)]
                soutr = [sm("soutr%d" % i, [32, 128]) for i in range(4)]
                rr = [0, 0]

                def load_T(dst, src_nat_ap, key):
                    i_ = rr[0] % 4
                    rr[0] += 1
                    bk_ = 6 + i_ % 2
                    p.dma("sp", natr[i_][:], src_nat_ap, writes=[("natr", i_)], key="s5natr%d" % i_)
                    p.op("pe", T(PS[bk_][:, 0:32], natr[i_][:], ident[0:32, 0:32]), reads=[("natr", i_), "ident"], writes=[("ps", bk_)])
                    p.op("dve", CP(dst, PS[bk_][:, 0:32]), reads=[("ps", bk_)], writes=[key])

                load_T(lr[:], ssm_lre[l].rearrange("(b g) p -> b (g p)", g=2), "lr")
                load_T(li[:], ssm_lim[l].rearrange("(b g) p -> b (g p)", g=2), "li")
                p.dma("sp", ldt[:], ssm_log_dt[l].rearrange("(b g) -> b g", g=2), writes=["ldt"], key="s5ldt")
                p.op("dve", CP(nat[:].rearrange("b (g p) -> b g p", g=2), ldt[:].unsqueeze(2).to_broadcast([32, 2, 64])),
                     reads=["ldt"], writes=["nat"])
                p.op("pe", T(PS[7][:, 0:32], nat[:], ident[0:32, 0:32]), reads=["nat", "ident"], writes=[("ps", 7)])
                p.op("act", ACT(dt_[:], PS[7][:, 0:32], AF.Exp), reads=[("ps", 7)], writes=["dt"])
                for b in range(NS):
                    load_T(SS[:, :, 0, b], state_re[l, b].rearrange("(b g) p -> b (g p)", g=2), "SS")
                    load_T(SS[:, :, 1, b], state_im[l, b].rearrange("(b g) p -> b (g p)", g=2), "SS")
                p.dma("sp", nat[0:8, :], ssm_d[l].rearrange("(f g) h -> f (g h)", g=8), writes=["nat"], key="s5nat")
                p.op("pe", T(PS[7][:, 0:8], nat[0:8, :], ident[0:8, 0:8]), reads=["nat", "ident"], writes=[("ps", 7)])
                p.op("dve", CP(dsk[:], PS[7][:, 0:8]), reads=[("ps", 7)], writes=["dsk"])

                V = lambda f, reads, writes: p.op("dve", f, reads=reads, writes=writes)
                V(TT(t1[:], lr[:], dt_[:], ALU.mult), ["lr", "dt"], ["t1"])
                for (dst, sgn, key) in ((mag, 1.0, "mag"), (imag, -1.0, "imag")):
                    V(TS(t2[:], t1[:], sgn / 6.0, 1.0, ALU.mult, ALU.add), ["t1"], ["t2"])
                    for kk in (5.0, 4.0, 3.0, 2.0, 1.0):
                        V(TT(t2[:], t2[:], t1[:], ALU.mult), ["t1", "t2"], ["t2"])
                        V(TS(t2[:], t2[:], sgn / kk, 1.0, ALU.mult, ALU.add), ["t2"], ["t2"])
                    V(CP(dst[:], t2[:]), ["t2"], [key])
                V(TT(t1[:], li[:], dt_[:], ALU.mult), ["li", "dt"], ["t1"])
                for _ in range(5):
                    V(TS(t2[:], t1[:], PI, 2.0 * PI, ALU.is_gt, ALU.mult), ["t1"], ["t2"])
                    V(TT(t1[:], t1[:], t2[:], ALU.subtract), ["t1", "t2"], ["t1"])
                p.op("act", ACT(sn[:], t1[:], AF.Sin), reads=["t1"], writes=["sn"])
                V(TS(t3[:], t1[:], PI / 2.0, None, ALU.add), ["t1"], ["t3"])
                V(TS(t2[:], t3[:], PI, 2.0 * PI, ALU.is_gt, ALU.mult), ["t3"], ["t2"])
                V(TT(t3[:], t3[:], t2[:], ALU.subtract), ["t3", "t2"], ["t3"])
                p.op("act", ACT(cs[:], t3[:], AF.Sin), reads=["t3"], writes=["cs"])
                V(TT(ab[:, :, 0], mag[:], cs[:], ALU.mult), ["mag", "cs"], ["ab"])
                V(TT(ab[:, :, 1], mag[:], sn[:], ALU.mult), ["mag", "sn"], ["ab"])
                V(TT(abi[:, :, 0], imag[:], cs[:], ALU.mult), ["imag", "cs"], ["abi"])
                V(TT(t2[:], imag[:], sn[:], ALU.mult), ["imag", "sn"], ["t2"])
                V(TS(abi[:, :, 1], t2[:], -1.0, None, ALU.mult), ["t2"], ["abi"])
                V(TT(t1[:], lr[:], lr[:], ALU.mult), ["lr"], ["t1"])
                V(TT(t2[:], li[:], li[:], ALU.mult), ["li"], ["t2"])
                V(TT(t1[:], t1[:], t2[:], ALU.add), ["t1", "t2"], ["t1"])
                V(lambda e: e.reciprocal(out=t1[:], in_=t1[:]), ["t1"], ["t1"])
                V(TS(t3[:], ab[:, :, 0], -1.0, None, ALU.add), ["ab"], ["t3"])
                V(TT(t2[:], t3[:], lr[:], ALU.mult), ["t3", "lr"], ["t2"])
                V(TT(czr[:], ab[:, :, 1], li[:], ALU.mult), ["ab", "li"], ["czr"])
                V(TT(czr[:], czr[:], t2[:], ALU.add), ["czr", "t2"], ["czr"])
                V(TT(czr[:], czr[:], t1[:], ALU.mult), ["czr", "t1"], ["czr"])
                V(TT(t2[:], ab[:, :, 1], lr[:], ALU.mult), ["ab", "lr"], ["t2"])
                V(TT(czi[:], t3[:], li[:], ALU.mult), ["t3", "li"], ["czi"])
                V(TT(czi[:], t2[:], czi[:], ALU.subtract), ["czi", "t2"], ["czi"])
                V(TT(czi[:], czi[:], t1[:], ALU.mult), ["czi", "t1"], ["czi"])

                def powers(dst, base, hb):
                    bs = base[:, hb * 16:(hb + 1) * 16, :]
                    V(CP(dst[:, :, :, 0], bs), ["ab", "abi"], ["E"])
                    m = 1
                    while m < 128:
                        pr = dst[:, :, 0, m - 1:m].to_broadcast([128, 16, m])
                        pi_ = dst[:, :, 1, m - 1:m].to_broadcast([128, 16, m])
                        er = dst[:, :, 0, 0:m]
                        ei = dst[:, :, 1, 0:m]
                        a1 = d1[:, :, 0:m]
                        a2 = d2[:, :, 0:m]
                        a3 = d3[:, :, 0:m]
                        a4 = d4[:, :, 0:m]
                        V(TT(a1, er, pr, ALU.mult), ["E"], ["d1"])
                        V(TT(a2, ei, pi_, ALU.mult), ["E"], ["d2"])
                        V(TT(a3, er, pi_, ALU.mult), ["E"], ["d3"])
                        V(TT(a4, ei, pr, ALU.mult), ["E"], ["d4"])
                        V(TT(dst[:, :, 0, m:2 * m], a1, a2, ALU.subtract), ["d1", "d2"], ["Er"])
                        V(TT(dst[:, :, 1, m:2 * m], a3, a4, ALU.add), ["d3", "d4"], ["Ei"])
                        p.last_w["E"] = p.last_w["Ei"]
                        m *= 2

                for hb in range(2):
                    powers(Ep, abi, hb)
                    for lb in range(16):
                        for pl in range(2):
                            bank = PS[6 + (lb // 2) % 2]
                            slot = (lb % 2) * 2 + pl
                            p.op("pe", T(bank[:, slot * 128:(slot + 1) * 128], Ep[:, lb, pl, :], ident[:]),
                                 reads=["E", "ident"], writes=[("ps", 6 + (lb // 2) % 2)])
                        if lb % 2 == 1:
                            p.op("dve", CP(EmT[:, lb - 1:lb + 1, :, :], bank[:, :].rearrange("p (a b c) -> p a b c", a=2, b=2)),
                                 reads=[("ps", 6 + (lb // 2) % 2)], writes=["EmT"])
                    powers(Ep, ab, hb)
                    gb = p.new_group()
                    for (dst, src) in ((bnr, ssm_bre), (bni, ssm_bim)):
                        for q4 in range(4):
                            p.dma("sp", dst[:, q4 * 4:(q4 + 1) * 4, :],
                                  src[l, hb * 32 + q4 * 8:hb * 32 + q4 * 8 + 8].rearrange("(b g) p h -> (g p) b h", g=2),
                                  writes=["bn"], key="s5b", group=gb)
                    zr = czr[:, hb * 16:(hb + 1) * 16].unsqueeze(2).to_broadcast([128, 16, 16])
                    zi = czi[:, hb * 16:(hb + 1) * 16].unsqueeze(2).to_broadcast([128, 16, 16])
                    V(TT(bbr[:], bnr[:], zr, ALU.mult), ["bn", "czr"], ["bbr"])
                    V(TT(bt[:], bni[:], zi, ALU.mult), ["bn", "czi"], ["bt"])
                    V(TT(bbr[:], bbr[:], bt[:], ALU.subtract), ["bbr", "bt"], ["bbr"])
                    V(TT(bbi[:], bni[:], zr, ALU.mult), ["bn", "czr"], ["bbi"])
                    V(TT(bt[:], bnr[:], zi, ALU.mult), ["bn", "czi"], ["bt"])
                    V(TT(bbi[:], bbi[:], bt[:], ALU.add), ["bbi", "bt"], ["bbi"])
                    for (pl, src, planes) in ((0, bbr, (0, 3)), (1, bbi, (1, 2))):
                        p.op("pool", lambda e: e.memset(XA[:], 0.0), reads=[], writes=["XA"])
                        for g2 in range(2):
                            for k in range(4):
                                V(CP(XA[g2 * 64:(g2 + 1) * 64, k::4, 32 * k + 16 * g2:32 * k + 16 * g2 + 16],
                                     src[g2 * 64:(g2 + 1) * 64, k::4, :]), ["bbr", "bbi"], ["XA"])
                        for lb in range(16):
                            bsel = 6 + (lb // 4) % 2
                            p.op("pe", T(PS[bsel][:, (lb % 4) * 128:(lb % 4 + 1) * 128], XA[:, lb, :], ident[:]),
                                 reads=["XA", "ident"], writes=[("ps", bsel)])
                            if lb % 4 == 3:
                                for pln in planes:
                                    p.op("act", ACT(Bw4[:, lb - 3:lb + 1, pln * 128:(pln + 1) * 128],
                                                    PS[bsel][:, :].rearrange("p (a b) -> p a b", a=4), AF.Copy),
                                         reads=[("ps", bsel)], writes=["Bw4"])
                    gc_ = p.new_group()
                    for (dst, src) in ((cnr, ssm_cre), (cni, ssm_cim)):
                        p.dma("sp", dst[:], src[l, hb * 32:(hb + 1) * 32].rearrange("(f g) h p -> (g h) f p", g=8),
                              writes=["cn"], key="s5c", group=gc_)
                    for (src, slots) in ((cnr, ((0, 1.0), (1, -1.0))), (cni, ((2, -1.0),))):
                        for g2 in range(2):
                            for k in range(4):
                                V(TS(XA[:, k::4, g2 * 64:(g2 + 1) * 64], src[:], maskg[:, 2 * k + g2:2 * k + g2 + 1], None, ALU.mult),
                                  ["cn", "maskg"], ["XA"])
                        for lb in range(16):
                            bsel = 6 + (lb // 4) % 2
                            p.op("pe", T(PS[bsel][:, (lb % 4) * 128:(lb % 4 + 1) * 128], XA[:, lb, :], ident[:]),
                                 reads=["XA", "ident"], writes=[("ps", bsel)])
                            if lb % 4 == 3:
                                for (sl, sc) in slots:
                                    p.op("act", ACT(Cw[:, lb - 3:lb + 1, sl, :],
                                                    PS[bsel][:, :].rearrange("p (a b) -> p a b", a=4), AF.Copy, scale=sc),
                                         reads=[("ps", bsel)], writes=["Cw"])
                    p.barrier()
                    if hb == 0:
                        arb = ab[:, :, 0:1].to_broadcast([128, 32, NS])
                        aib = ab[:, :, 1:2].to_broadcast([128, 32, NS])
                        V(TT(sc1[:], SS[:, :, 0, :], arb, ALU.mult), ["SS", "ab"], ["sc1"])
                        V(TT(sc2[:], SS[:, :, 1, :], aib, ALU.mult), ["SS", "ab"], ["sc2"])
                        V(TT(CPs[:, :, 0, :], sc1[:], sc2[:], ALU.subtract), ["sc1", "sc2"], ["CPs"])
                        V(TT(sc1[:], SS[:, :, 1, :], arb, ALU.mult), ["SS", "ab"], ["sc1"])
                        V(TT(sc2[:], SS[:, :, 0, :], aib, ALU.mult), ["SS", "ab"], ["sc2"])
                        V(TT(CPs[:, :, 1, :], sc1[:], sc2[:], ALU.add), ["sc1", "sc2"], ["CPs"])
                    BB = (0, 1, 6)
                    PB = (2, 3, 7)
                    QPOOL = False
                    bqs = [d1[:, 0:8, :].rearrange("p a b -> p (a b)"), d1[:, 8:16, :].rearrange("p a b -> p (a b)"),
                           d2[:, 0:8, :].rearrange("p a b -> p (a b)")]
                    iters = []
                    gi = 0
                    for f4 in range(4):
                        ft = hb * 4 + f4
                        for si in range(17):
                            for k in range(4):
                                iters.append((f4, ft, si, k, gi, len(iters)))
                            gi += 1

                    def st1(itd):
                        f4, ft, si, k, gi, i = itd
                        lb = f4 * 4 + k
                        bk = BB[i % 3]
                        if si < 16:
                            c0 = si * 128
                            p.op("pe", MM(PS[bk][:, :], xn[:, ft, c0:c0 + 128], Bw4[:, lb, :], True, True),
                                 reads=[("xn", ft, si // 4), "Bw4"], writes=[("ps", bk)])
                        else:
                            for pl in range(2):
                                p.op("pe", MM(PS[bk][:, pl * NS:(pl + 1) * NS], Bw4[:, lb, pl * 128:(pl + 1) * 128], xn[:, ft, SEQ:NT], True, True),
                                     reads=[("xn", ft, 4), "Bw4"], writes=[("ps", bk)])

                    def st2(itd):
                        f4, ft, si, k, gi, i = itd
                        lb = f4 * 4 + k
                        blk = hb * 16 + lb
                        bk = BB[i % 3]
                        if si < 16:
                            q = qb[i % 3]
                            if QPOOL and i % 3 == 2:
                                bq = bqs[i % 3]
                                p.op("act", ACT(bq, PS[bk][:, :], AF.Copy), reads=[("ps", bk)], writes=[("bq", i % 3)])
                                p.op("pool", TT(q[:, :].rearrange("p (a b) -> p a b", a=2), bq.rearrange("p (a b) -> p a b", a=2),
                                                EmT[:, lb:lb + 1, :, :].rearrange("p a b c -> p a (b c)").to_broadcast([128, 2, 256]), ALU.mult),
                                     reads=[("bq", i % 3), "EmT"], writes=[("q", i % 3)])
                            else:
                                V(TT(q[:, :].rearrange("p (a b) -> p a b", a=2), PS[bk][:, :].rearrange("p (a b) -> p a b", a=2),
                                     EmT[:, lb:lb + 1, :, :].rearrange("p a b c -> p a (b c)").to_broadcast([128, 2, 256]), ALU.mult),
                                  [("ps", bk), "EmT"], [("q", i % 3)])
                        else:
                            V(TT(NSt[:, blk, :, :], PS[bk][:, 0:2 * NS].rearrange("p (a b) -> p a b", a=2), CPs[:, blk, :, :], ALU.add),
                              [("ps", bk), "CPs"], [("NSt", blk)])
                            V(CP(wS[i % 3][:], NSt[:, blk, :, :]), [("NSt", blk)], [("wS", i % 3)])

                    def st3(itd):
                        f4, ft, si, k, gi, i = itd
                        if si >= 16:
                            return
                        q = qb[i % 3]
                        PP = PS[PB[i % 3]]
                        rk, wk = [("q", i % 3), "tri"], [("ps", PB[i % 3])]
                        p.op("pe", MM(PP[:, 0:128], q[:, 0:128], tri_bf[:, 0:128], True, False), reads=rk, writes=wk)
                        p.op("pe", MM(PP[:, 0:128], q[:, 128:256], tri_bf[:, 128:256], False, True), reads=rk, writes=wk)
                        p.op("pe", MM(PP[:, 128:256], q[:, 256:384], tri_bf[:, 0:128], True, False), reads=rk, writes=wk)
                        p.op("pe", MM(PP[:, 128:256], q[:, 384:512], tri_bf[:, 0:128], False, True), reads=rk, writes=wk)

                    def st4(itd):
                        f4, ft, si, k, gi, i = itd
                        if si >= 16:
                            return
                        lb = f4 * 4 + k
                        blk = hb * 16 + lb
                        PP = PS[PB[i % 3]]
                        pk = ("ps", PB[i % 3])
                        w = wb[i % 3]
                        ck = ("carryP", blk)
                        cki = ("carryPi", blk)
                        V(STT(w[:, 0:2, :], PP[:, 0:128].unsqueeze(1).to_broadcast([128, 2, 128]), carryP[:, blk, 0:1],
                              Ep[:, lb, :, :], ALU.add, ALU.mult), [pk, "E", ck], [("w", i % 3)])
                        V(STT(w[:, 2:4, :], PP[:, 128:256].unsqueeze(1).to_broadcast([128, 2, 128]), carryP[:, blk, 1:2],
                              Ep[:, lb, :, :], ALU.add, ALU.mult), [pk, "E", cki], [("w2", i % 3)])
                        V(TT(cab4[:, k, :], PP[:, 127:256:128], carryP[:, blk, :], ALU.add), [pk, ck, cki], [("cab4", k)])
                        if k == 3:
                            lb0 = f4 * 4
                            blk0 = hb * 16 + lb0
                            E4 = Ep[:, lb0:lb0 + 4, :, 127]
                            cks = [("carryP", blk0 + kk) for kk in range(4)]
                            c4 = [("cab4", kk) for kk in range(4)]
                            V(TT(ct4[:], E4, cab4[:, :, 0:1].to_broadcast([128, 4, 2]), ALU.mult), c4 + ["E"], ["ct4"])
                            V(TT(cu4[:], E4, cab4[:, :, 1:2].to_broadcast([128, 4, 2]), ALU.mult), c4 + ["E"], ["cu4"])
                            V(TT(carryP[:, blk0:blk0 + 4, 0], ct4[:, :, 0], cu4[:, :, 1], ALU.subtract), ["ct4", "cu4"], cks)
                            V(TT(carryP[:, blk0:blk0 + 4, 1], ct4[:, :, 1], cu4[:, :, 0], ALU.add), ["ct4", "cu4"], [("carryPi", blk0 + kk) for kk in range(4)])

                    def st5(itd):
                        f4, ft, si, k, gi, i = itd
                        lb = f4 * 4 + k
                        ybank = 4 + gi % 2
                        if si < 16:
                            c0, L, ci = si * 128, 128, si // 4
                            w = wb[i % 3]
                            for wi, (pl, sl) in enumerate(((0, 0), (3, 1), (2, 2), (1, 2))):
                                p.op("pe", MM(PS[ybank][:, 0:L], Cw[:, lb, sl, :], w[:, pl, :], k == 0 and wi == 0, k == 3 and wi == 3),
                                     reads=[("w", i % 3), ("w2", i % 3), "Cw"], writes=[("ps", ybank)])
                        else:
                            c0, L, ci = SEQ, NS, 4
                            for wi, (pl, sl) in enumerate(((0, 0), (1, 2))):
                                p.op("pe", MM(PS[ybank][:, 0:L], Cw[:, lb, sl, :], wS[i % 3][:, pl, :], k == 0 and wi == 0, k == 3 and wi == 1),
                                     reads=[("wS", i % 3), "Cw"], writes=[("ps", ybank)])
                        if k == 3:
                            yy = yv[gi % 2]
                            V(STT(yy[:, 0:L], xn[:, ft, c0:c0 + L], dsk[:, ft:ft + 1], PS[ybank][:, 0:L], ALU.mult, ALU.add),
                              [("ps", ybank), ("xn", ft, ci), "dsk"], [("yv", gi % 2)])
                            p.op("act", ACT(xn[:, ft, c0:c0 + L], yy[:, 0:L], AF.Gelu_apprx_tanh),
                                 reads=[("yv", gi % 2)], writes=[("xn", ft, ci)])

                    NI = len(iters)
                    for s_ in range(NI + 2):
                        if s_ < NI:
                            st1(iters[s_])
                            st2(iters[s_])
                        if 0 <= s_ - 1 < NI:
                            st3(iters[s_ - 1])
                            st4(iters[s_ - 1])
                        if 0 <= s_ - 2 < NI:
                            st5(iters[s_ - 2])
                def store_state(src_ap, dst_ap, rkey):
                    i_ = rr[1] % 4
                    rr[1] += 1
                    bk_ = 6 + i_ % 2
                    p.op("pe", T(PS[bk_][0:32, 0:128], src_ap, ident[:]), reads=[rkey, "ident"], writes=[("ps", bk_)])
                    p.op("dve", CP(soutr[i_][:], PS[bk_][0:32, 0:128]), reads=[("ps", bk_)], writes=[("soutr", i_)])
                    p.dma("sp", dst_ap.rearrange("(b g) p -> b (g p)", g=2), soutr[i_][:], reads=[("soutr", i_)], key="soutr%d" % i_)
                p.barrier()
                store_state(carryP[:, :, 0], o_ssm_re_p[l], "carryAll")
                store_state(carryP[:, :, 1], o_ssm_im_p[l], "carryAll")
                for b in range(NS):
                    store_state(NSt[:, :, 0, b], o_ssm_re_s[l, b], "carryAll")
                    store_state(NSt[:, :, 1, b], o_ssm_im_s[l, b], "carryAll")
                if l + 1 < NA:
                    p.op("dve", lambda e: e.memset(carryP[:], 0.0), writes=["carryAll"])
                p.barrier()
            with ExitStack() as ph:
                alloc_w(ph)
                glbp = sbt(ph, "glbp", [128, 16], F32)
                gt = [sbt(ph, "gt%d" % i, [128, 512], F32) for i in range(2)]
                W8, sA = B["W8"], B["sA"]
                glnat = sbt(ph, "glnat", [16, 128], F32)
                p.dma("sp", glnat[:], glu_b[l].rearrange("(f p) -> f p", p=128), writes=["gnat2"], key="glb")
                p.op("pe", T(PS[7][:, 0:16], glnat[:], ident[0:16, 0:16]), reads=["gnat2", "ident"], writes=[("ps", 7)])
                p.op("dve", CP(glbp[:], PS[7][:, 0:16]), reads=[("ps", 7)], writes=["glbp"])
                for c4 in range(2):
                    s1 = load_w8(glu_w[l][:, c4 * 512:(c4 + 1) * 512])
                    s2 = load_w8(glu_w[l][:, D + c4 * 512:D + (c4 + 1) * 512])
                    for j in range(4):
                        fo = c4 * 4 + j
                        for ci, (c0, nn) in enumerate(CBS):
                            u = unit_ctr[0]
                            unit_ctr[0] += 1
                            ba, bb = u % 2, 2 + u % 2
                            for kt in range(8):
                                p.op("pe", MM(PS[ba][:, 0:nn], W8[s1][:, kt, j * 128:(j + 1) * 128], xn[:, kt, c0:c0 + nn], kt == 0, kt == 7),
                                     reads=[("w8", s1), ("xn", kt, ci)], writes=[("ps", ba)])
                            for kt in range(8):
                                p.op("pe", MM(PS[bb][:, 0:nn], W8[s2][:, kt, j * 128:(j + 1) * 128], xn[:, kt, c0:c0 + nn], kt == 0, kt == 7),
                                     reads=[("w8", s2), ("xn", kt, ci)], writes=[("ps", bb)])
                            p.op("act", ACT(sA[u % 2][:, 0:nn], PS[bb][:, 0:nn], AF.Sigmoid, bias=glbp[:, 8 + fo:9 + fo]),
                                 reads=[("ps", bb), "glbp"], writes=[("sA", u % 2)])
                            p.op("dve", STT(gt[u % 2][:, 0:nn], PS[ba][:, 0:nn], glbp[:, fo:fo + 1], sA[u % 2][:, 0:nn], ALU.add, ALU.mult),
                                 reads=[("ps", ba), ("sA", u % 2), "glbp"], writes=[("gt", u % 2)])
                            p.op("dve", TT(resid[:, fo, c0:c0 + nn], resid[:, fo, c0:c0 + nn], gt[u % 2][:, 0:nn], ALU.add),
                                 reads=[("gt", u % 2), ("resid", fo, ci)], writes=[("resid", fo, ci)])
                p.barrier()

        DILS = (1, 4, 16)
        WINS = (128, 512, 2048)

        def kv_phase():
            with ExitStack() as ph:
                alloc_norm(ph)
                alloc_w(ph)
                W8 = B["W8"]
                stg = [sbt(ph, "kvstg%d" % i, [128, 512], F32) for i in range(2)]
                ktst = [sbt(ph, "ktst%d" % i, [128, NT], BF16) for i in range(2)]
                rmsnorm(12)
                tiles = [(tt * 128, 128, tt // 4) for tt in range(16)] + [(SEQ, NS, 4)]
                it = 0
                for c6 in range(6):
                    g, kv = c6 // 2, c6 % 2
                    sl = load_w8(attn_w_kv[:, c6 * 512:(c6 + 1) * 512])
                    for (c0, M, ci) in tiles:
                        if kv == 0 and c0 < SEQ - WINS[g]:
                            continue
                        bk = it % 2
                        it += 1
                        for kt in range(8):
                            p.op("pe", MM(PS[bk][0:M, :], xn[:, kt, c0:c0 + M], W8[sl][:, kt, :], kt == 0, kt == 7),
                                 reads=[("w8", sl), ("xn", kt, ci)], writes=[("ps", bk)])
                        p.op("act", ACT(stg[bk][0:M, :], PS[bk][0:M, :], AF.Copy), reads=[("ps", bk)], writes=[("stg", bk)])
                        p.dma("sp", kvd[g, kv, c0:c0 + M, :], stg[bk][0:M, :], reads=[("stg", bk)], writes=[("kvd", g, kv, c0)], key="stg%d" % bk)
                for g in range(3):
                    sl = load_w8(attn_w_kv[:, g * 1024:g * 1024 + 512])
                    for j in range(4):
                        st_ = ktst[j % 2]
                        for ci, (c0, nn) in enumerate(CBS):
                            bk = 2 + it % 2
                            it += 1
                            for kt in range(8):
                                p.op("pe", MM(PS[bk][:, 0:nn], W8[sl][:, kt, j * 128:(j + 1) * 128], xn[:, kt, c0:c0 + nn], kt == 0, kt == 7),
                                     reads=[("w8", sl), ("xn", kt, ci)], writes=[("ps", bk)])
                            p.op("act", ACT(st_[:, c0:c0 + nn], PS[bk][:, 0:nn], AF.Copy), reads=[("ps", bk)], writes=[("ktst", j % 2)])
                        p.dma("sp", ktd[g, j], st_[:], reads=[("ktst", j % 2)], writes=[("ktd", g, j)], key="ktst%d" % (j % 2))
                p.barrier()
                go = p.new_group()
                outs = ((0, o_kv128_p, o_kv128_s), (1, o_kv512_p, o_kv512_s), (2, o_kv2048_p, o_kv2048_s))
                for (g, op_, os_) in outs:
                    w_ = WINS[g]
                    for kv in range(2):
                        p.dma("sp", op_[:, kv, :], kvd[g, kv, SEQ - w_:SEQ, :], key="kvout", group=go)
                        p.dma("sp", os_[:, kv, :], kvd[g, kv, SEQ:NT, :], key="kvout", group=go)
                p.barrier()

        def attn_layer(b_):
            layer = NA + b_
            with ExitStack() as ph:
                alloc_norm(ph)
                rmsnorm(4 + layer)
                p.barrier()
            with ExitStack() as ph:
                B["W8"] = [sbt(ph, "aw8_%d" % i, [128, 8, 256], BF16) for i in range(2)]
                B["WO"] = [sbt(ph, "awo_%d" % i, [128, 4, 1024], BF16) for i in range(1)]
                B["hbuf"] = sbt(ph, "ahbuf", [128, 4, NT], BF16)
                W8, WO = B["W8"], B["WO"]
                sm = lambda name, shape, dt=F32: sbt(ph, "at_" + name, shape, dt)
                qT = sm("qT", [128, 2, 2, NT], BF16)
                kT = sm("kT", [128, 2, NT], BF16)
                ACC4 = sm("ACC4", [128, 4, NT])
                mrg = B["hbuf"]
                bm = sm("bm", [128, 2, 512]); bmc = sm("bmc", [128, 4])
                Vt = [sm("V%d" % i, [128, 2, 4, 64], BF16) for i in range(4)]
                for i_ in range(4):
                    p.op("dve", (lambda t_: lambda e: e.memset(t_[:, :, 1:3, :], 1.0))(Vt[i_]), writes=[("V", i_)])
                sbuf_s = [sm("ss%d" % i, [128, 512]) for i in range(4)]
                pT = [sm("pT%d" % i, [128, 512], BF16) for i in range(4)]
                p.op("dve", lambda e: e.memset(qT[:], 0.0), writes=["qT"])
                kcf = sm("kcf", [128, 256]); kcT = sm("kcT", [128, 2, 128], BF16)
                ones64 = ones_bf[:, 0:64]
                it = [0]
                vctr = [0]
                qslots = {}

                SB = (0, 1, 6, 7)
                pend = [None]

                def unit_back(st):
                    bufs, nq, acc_cols, par = st
                    nkb = len(bufs)
                    ab_ = 2 + par
                    AL = PS[ab_]
                    for h4 in range(4):
                        hp2, hh = h4 // 2, h4 % 2
                        for kbi, (u, vt, vkey, nk) in enumerate(bufs):
                            p.op("pe", MM(AL[:, h4 * nq:(h4 + 1) * nq], vt[0:nk, hp2, 2 * hh:2 * hh + 2, :],
                                          pT[u][0:nk, h4 * nq:(h4 + 1) * nq], kbi == 0, kbi == nkb - 1),
                                 reads=[("pT", u), vkey], writes=[("ps", ab_)])
                    d_ = acc_cols(ACC4)
                    p.op("dve", TT(d_, d_, AL[:, 0:4 * nq].rearrange("p (a b) -> p a b", a=4), ALU.add),
                         reads=[("ps", ab_), "acc"], writes=["acc"])

                def flush():
                    if pend[0] is not None:
                        unit_back(pend[0])
                        pend[0] = None

                def unit(g, ps_, keysets, qap, nq, acc_cols):
                    par = it[0] % 2
                    it[0] += 1
                    bufs = []
                    for kbi, (kfn, vt, vkey, nk, bias) in enumerate(keysets):
                        u = par * 2 + kbi
                        bank = PS[SB[u]]
                        for h4 in range(4):
                            hp2, hh = h4 // 2, h4 % 2
                            p.op("pe", MM(bank[0:nk, h4 * nq:(h4 + 1) * nq], kfn(hh, hp2), qap(hh, hp2), True, True),
                                 reads=["qT", "kT", "kcT"], writes=[("ps", SB[u])])
                        if bias is not None:
                            p.op("dve", STT(sbuf_s[u][0:nk, 0:4 * nq], bank[0:nk, 0:4 * nq], 0.125, bias, ALU.mult, ALU.add),
                                 reads=[("ps", SB[u]), "bm"], writes=[("ss", u)])
                        else:
                            p.op("dve", TS(sbuf_s[u][0:nk, 0:4 * nq], bank[0:nk, 0:4 * nq], 0.125, None, ALU.mult),
                                 reads=[("ps", SB[u])], writes=[("ss", u)])
                        p.op("act", ACT(pT[u][0:nk, 0:4 * nq], sbuf_s[u][0:nk, 0:4 * nq], AF.Exp), reads=[("ss", u)], writes=[("pT", u)])
                        bufs.append((u, vt, vkey, nk))
                    st = (bufs, nq, acc_cols, par)
                    flush()
                    pend[0] = st

                for ps_ in range(2):
                    p.op("pool", lambda e: e.memset(ACC4[:], 0.0), writes=["acc"])
                    for g in range(3):
                        d, w_ = DILS[g], WINS[g]
                        oi = ps_ * 3 + g
                        if oi == 0:
                            qslots[0] = load_w8(attn_w_q[b_][:, 0:256])
                        sl = qslots[oi]
                        for j in range(2):
                            for ci, (c0, nn) in enumerate(CBS):
                                bk = 4 + it[0] % 2
                                it[0] += 1
                                for kt in range(8):
                                    p.op("pe", MM(PS[bk][:, 0:nn], W8[sl][:, kt, j * 128:(j + 1) * 128], xn[:, kt, c0:c0 + nn], kt == 0, kt == 7),
                                         reads=[("w8", sl), ("xn", kt, ci)], writes=[("ps", bk)])
                                p.op("act", ACT(qT[0:64, 0, j, c0:c0 + nn], PS[bk][0:64, 0:nn], AF.Copy), reads=[("ps", bk)], writes=["qT"])
                                p.op("act", ACT(qT[64:128, 1, j, c0:c0 + nn], PS[bk][64:128, 0:nn], AF.Copy), reads=[("ps", bk)], writes=["qT"])
                        if oi + 1 < 6:
                            ps_n, g_n = (oi + 1) // 3, (oi + 1) % 3
                            qslots[oi + 1] = load_w8(attn_w_q[b_][:, g_n * 512 + ps_n * 256:g_n * 512 + ps_n * 256 + 256])
                        gk = p.new_group()
                        for j in range(2):
                            p.dma("sp", kT[:, j, :], ktd[g, ps_ * 2 + j], writes=["kT"], key="kTld", group=gk)
                        gbm = p.new_group()
                        p.dma("sp", bm[:], c_bias[g, ps_].rearrange("k p c -> p k c"), writes=["bm"], key="bmld", group=gbm)
                        p.dma("sp", bmc[:], c_biasc[g, ps_], writes=["bm"], key="bmld", group=gbm)
                        nres = SEQ // d
                        nb = nres // 128
                        for r in range(d):
                            prev = None
                            for n in range(nb):
                                start = r + d * 128 * n
                                cols = slice(start, start + 127 * d + 1, d)
                                vi = vctr[0] % 4
                                vctr[0] += 1
                                gv = p.new_group()
                                vsrc = kvd[g, 1, start:start + 127 * d + 1:d, ps_ * 256:(ps_ + 1) * 256].rearrange("k (a b e) -> k a b e", a=2, b=2)
                                for hh_ in range(2):
                                    p.dma("pool", Vt[vi][:, :, 3 * hh_, :], vsrc[:, :, hh_, :], writes=[("V", vi)], key="V%d" % vi, group=gv)
                                kfn = (lambda cs: lambda hh, hp2: kT[:, hp2, cs])(cols)
                                qfn = (lambda cs: lambda hh, hp2: qT[:, hh, hp2, cs])(cols)
                                cur = (kfn, Vt[vi], ("V", vi), 128, bm[:, 1, :])
                                ks = [cur] if prev is None else [(prev[0], prev[1], prev[2], 128, bm[:, 0, :]), cur]
                                unit(g, ps_, ks, qfn, 128, (lambda cs: lambda dst: dst[:, :, cs])(cols))
                                prev = cur
                        for b in range(NS):
                            vi = vctr[0] % 4
                            vctr[0] += 1
                            vj = vctr[0] % 4
                            vctr[0] += 1
                            cch = caches[g]
                            p.dma("sp", kcf[:], cch[b, 0:w_ - d + 1:d, 0, ps_ * 256:(ps_ + 1) * 256], writes=["kcf"], key="kcf")
                            gv1, gv2 = p.new_group(), p.new_group()
                            vs1 = cch[b, 0:w_ - d + 1:d, 1, ps_ * 256:(ps_ + 1) * 256].rearrange("k (a b e) -> k a b e", a=2, b=2)
                            vs2 = kvd[g, 1, SEQ + b:SEQ + b + 1, ps_ * 256:(ps_ + 1) * 256].rearrange("k (a b e) -> k a b e", a=2, b=2)
                            for hh_ in range(2):
                                p.dma("pool", Vt[vi][:, :, 3 * hh_, :], vs1[:, :, hh_, :], writes=[("V", vi)], key="V%d" % vi, group=gv1)
                                p.dma("pool", Vt[vj][0:1, :, 3 * hh_, :], vs2[:, :, hh_, :], writes=[("V", vj)], key="V%d" % vj, group=gv2)
                            for j in range(2):
                                p.op("pe", T(PS[4][:, j * 128:(j + 1) * 128], kcf[:, j * 128:(j + 1) * 128], ident[:]),
                                     reads=["kcf", "ident"], writes=[("ps", 4)])
                            p.op("dve", CP(kcT[:], PS[4][:, 0:256].rearrange("p (a b) -> p a b", a=2)), reads=[("ps", 4)], writes=["kcT"])
                            col = SEQ + b
                            ks = [((lambda hh, hp2: kcT[:, hp2, :]), Vt[vi], ("V", vi), 128, bmc[:]),
                                  ((lambda c_: lambda hh, hp2: kT[:, hp2, c_:c_ + 1])(col), Vt[vj], ("V", vj), 1, None)]
                            qfn = (lambda c_: lambda hh, hp2: qT[:, hh, hp2, c_:c_ + 1])(col)
                            unit(g, ps_, ks, qfn, 1, (lambda c_: lambda dst: dst[:, :, c_:c_ + 1])(col))
                        flush()
                    flush()
                    mi = 0
                    for hp2 in range(2):
                        for ci, (c0, nn) in enumerate(CBS):
                            bk = 4 + mi % 2
                            tl, tr = sbuf_s[mi % 2], sbuf_s[2 + mi % 2]
                            mi += 1
                            p.op("pe", MM(PS[bk][0:64, 0:nn], ident[:, 64:128], ACC4[:, 2 * hp2, c0:c0 + nn], True, True),
                                 reads=["acc", "ident"], writes=[("ps", bk)])
                            p.op("pe", MM(PS[bk][64:128, 0:nn], ident[:, 0:64], ACC4[:, 2 * hp2 + 1, c0:c0 + nn], True, True),
                                 reads=["acc", "ident"], writes=[("ps", bk)])
                            p.op("act", ACT(tl[:, 0:nn], PS[bk][:, 0:nn], AF.Ln), reads=[("ps", bk)], writes=[("ss", mi % 2)])
                            p.op("act", ACT(tr[:, 0:nn], tl[:, 0:nn], AF.Exp, scale=-1.0), reads=[("ss", mi % 2)], writes=[("ss", 2 + mi % 2)])
                            hp = ps_ * 2 + hp2
                            p.op("dve", TT(mrg[0:64, hp, c0:c0 + nn], ACC4[0:64, 2 * hp2, c0:c0 + nn], tr[0:64, 0:nn], ALU.mult),
                                 reads=["acc", ("ss", 2 + mi % 2)], writes=[("h", hp, ci)])
                            p.op("dve", TT(mrg[64:128, hp, c0:c0 + nn], ACC4[64:128, 2 * hp2 + 1, c0:c0 + nn], tr[64:128, 0:nn], ALU.mult),
                                 reads=["acc", ("ss", 2 + mi % 2)], writes=[("h", hp, ci)])
                so_ = load_wo(attn_w_o[b_])
                for fo in range(8):
                    for ci, (c0, nn) in enumerate(CBS):
                        dsel = 4 + dn_ctr[0] % 2
                        dn_ctr[0] += 1
                        for j in range(4):
                            p.op("pe", MM(PS[dsel][:, 0:nn], WO[so_][:, j, fo * 128:(fo + 1) * 128], mrg[:, j, c0:c0 + nn], j == 0, j == 3),
                                 reads=[("wo", so_), ("h", j, ci)], writes=[("ps", dsel)])
                        p.op("dve", TT(resid[:, fo, c0:c0 + nn], resid[:, fo, c0:c0 + nn], PS[dsel][:, 0:nn], ALU.add),
                             reads=[("ps", dsel), ("resid", fo, ci)], writes=[("resid", fo, ci)])
                p.barrier()

        if stage == "A":
            ffn_phase(0, 0, 0)
        elif stage == "B":
            s5_layer(0)
        elif stage == "C":
            kv_phase()
            attn_layer(0)
        elif stage == "full":
            for l in range(NA):
                ffn_phase(l, 0, l)
                s5_layer(l)
                ffn_phase(l, 1, 8 + l)
            kv_phase()
            for b_ in range(2):
                ffn_phase(NA + b_, 0, NA + b_)
                attn_layer(b_)
                ffn_phase(NA + b_, 1, 8 + NA + b_)

        if debug:
            p.dma("sp", dbg, resid[:], reads=ALLRES, key="dbg")

        with ExitStack() as ph:
            alloc_norm(ph)
            xtok = [sbt(ph, "xtok%d" % i, [128, 1024], F32) for i in range(4)]
            rmsnorm(13, out_f32=True)
            for tt in range(16):
                s = tt % 4
                for half in range(2):
                    bank = PS[6 + half]
                    for f4 in range(4):
                        ft = half * 4 + f4
                        p.op("pe", T(bank[:, f4 * 128:(f4 + 1) * 128], resid[:, ft, tt * 128:(tt + 1) * 128], ident[:]),
                             reads=[("resid", ft, tt // 4), "ident"], writes=[("ps", 6 + half)])
                    p.op("act" if half == 0 else "dve",
                         ACT(xtok[s][:, half * 512:(half + 1) * 512], bank[:, :], AF.Copy) if half == 0
                         else CP(xtok[s][:, half * 512:(half + 1) * 512], bank[:, :]),
                         reads=[("ps", 6 + half)], writes=[("xtok", s, half)])
                p.dma("sp", y_prompt[tt * 128:(tt + 1) * 128, :], xtok[s][:], reads=[("xtok", s, 0), ("xtok", s, 1)], key="xtok%d" % s)
            for ft in range(8):
                p.op("pe", T(PS[6][0:NS, ft * 128:(ft + 1) * 128] if ft < 4 else PS[7][0:NS, (ft - 4) * 128:(ft - 3) * 128],
                             resid[:, ft, SEQ:NT], ident[:]),
                     reads=[("resid", ft, 4), "ident"], writes=[("ps", 6 if ft < 4 else 7)])
            for half in range(2):
                p.op("act", ACT(xtok[0][0:NS, half * 512:(half + 1) * 512], PS[6 + half][0:NS, :], AF.Copy),
                     reads=[("ps", 6 + half)], writes=[("xtok", 0, half)])
            p.dma("sp", y_sample, xtok[0][0:NS, :], reads=[("xtok", 0, 0), ("xtok", 0, 1)], key="xtok0")

        p.emit()
    return nc


def _alibi():
    n = 24
    s = np.exp2(-8.0 * np.arange(1, n + 1, dtype=np.float64) / n)
    return s.reshape(3, 8)


def host_consts():
    f = np.float32
    tri = (np.arange(128)[:, None] <= np.arange(128)[None, :]).astype(f)
    sl = _alibi()
    dils = (1, 4, 16)
    k = np.arange(128)[:, None].astype(np.float64)
    q = np.arange(128)[None, :].astype(np.float64)
    cb = np.zeros((3, 2, 2, 128, 4, 128), np.float64)
    cc = np.zeros((3, 2, 128, 4), np.float64)
    for g in range(3):
        for ps in range(2):
            for h4 in range(4):
                s_ = sl[g, ps * 4 + h4] * dils[g]
                dprev = 128 + q - k
                cb[g, ps, 0, :, h4, :] = np.where(dprev <= 128, -s_ * dprev, -30000.0)
                dcur = q - k
                cb[g, ps, 1, :, h4, :] = np.where(dcur >= 0, -s_ * dcur, -30000.0)
                cc[g, ps, :, h4] = -s_ * (128 - np.arange(128))
    return {
        "c_ident": np.eye(128, dtype=f),
        "c_tri": np.concatenate([tri, -tri], axis=1).astype(f),
        "c_maskg": (np.arange(128)[:, None] // 16 == np.arange(8)[None, :]).astype(f),
        "c_bias": cb.reshape(3, 2, 2, 128, 512).astype(f),
        "c_biasc": cc.astype(f),
    }


_NC_CACHE = {}


def kernel(**inputs):
    if "nc" not in _NC_CACHE:
        _NC_CACHE["nc"] = build("full", debug=False)
    nc = _NC_CACHE["nc"]
    consts = host_consts()
    A = {k: np.ascontiguousarray(np.asarray(v)) for k, v in inputs.items()}
    in_maps = []
    for c in range(8):
        m = dict(consts)
        for k, v in A.items():
            if k == "x_prompt":
                m[k] = np.ascontiguousarray(v[c])
            elif k == "x_sample":
                m[k] = np.ascontiguousarray(v[4 * c:4 * c + 4, 0])
            elif k in ("state_ssm_re", "state_ssm_im"):
                m[k] = np.ascontiguousarray(v[:, 4 * c:4 * c + 4])
            elif k.startswith("cache_kv"):
                m[k] = np.ascontiguousarray(v[4 * c:4 * c + 4].reshape(4, v.shape[1], 2, 512))
            else:
                m[k] = v
        in_maps.append(m)
    res = run_bass_kernel_spmd(nc, in_maps, core_ids=list(range(8)))
    R = res.results
    f = np.float32
    y_prompt = np.stack([R[c]["y_prompt"] for c in range(8)]).astype(f)
    y_sample = np.concatenate([R[c]["y_sample"] for c in range(8)])[:, None, :].astype(f)
    srp = np.stack([R[c]["ssm_re_prompt"] for c in range(8)], axis=1).astype(f)
    sip = np.stack([R[c]["ssm_im_prompt"] for c in range(8)], axis=1).astype(f)
    srs = np.concatenate([R[c]["ssm_re_sample"] for c in range(8)], axis=1).astype(f)
    sis = np.concatenate([R[c]["ssm_im_sample"] for c in range(8)], axis=1).astype(f)
    kvp = [np.stack([R[c][n].reshape(-1, 2, 8, 64) for c in range(8)]).astype(f)
           for n in ("kv_w128_prompt", "kv_w512_prompt", "kv_w2048_prompt")]
    kvs = [np.concatenate([R[c][n].reshape(4, 1, 2, 8, 64) for c in range(8)]).astype(f)
           for n in ("kv_w128_sample", "kv_w512_sample", "kv_w2048_sample")]
    return (y_prompt, y_sample, srp, sip, kvp[0], kvp[1], kvp[2], srs, sis, kvs[0], kvs[1], kvs[2])
```
